# Optimizing a Trainium2 kernel written in Bass

```python
import math
import jax, jax.numpy as jnp
from jax import lax
import numpy as np

D_MODEL = 1024
BATCH = 8
SEQ = 2048
DEPTH = 1

GRID_W = 64
CTX_LEN = 256
D_CONV = 512
CONV_K = 3
MLA_HEADS = 8
QK_NOPE = 64
QK_ROPE = 32
V_DIM = 64
D_MLA = MLA_HEADS * V_DIM
Q_LORA = 256
KV_LORA = 128
ROPE_THETA = 10000.0
Q_BLOCK = 128
N_BRANCH = 2
LN_EPS = 1e-5
RMS_EPS = 1e-6
DEEPNORM_ALPHA = (2.0 * DEPTH) ** 0.25
DEEPNORM_BETA = (8.0 * DEPTH) ** -0.25
N_IN = 4 * D_CONV + Q_LORA + KV_LORA + QK_ROPE + D_MLA + N_BRANCH * D_MODEL

kernel_name = "hybrid_conv_mla_prefix_dit_block"


def _layer_norm(t, g, b):
    tf = t.astype(jnp.float32)
    mu = jnp.mean(tf, axis=-1, keepdims=True)
    var = jnp.mean(jnp.square(tf - mu), axis=-1, keepdims=True)
    y = (tf - mu) * lax.rsqrt(var + LN_EPS)
    return (y * g.astype(jnp.float32) + b.astype(jnp.float32)).astype(t.dtype)


def _rms_norm(t, g):
    tf = t.astype(jnp.float32)
    y = tf * lax.rsqrt(jnp.mean(jnp.square(tf), axis=-1, keepdims=True) + RMS_EPS)
    return (y * g.astype(jnp.float32)).astype(t.dtype)


def _split_proj(p):
    sizes = (D_CONV, D_CONV, D_CONV, D_CONV, Q_LORA, KV_LORA, QK_ROPE, D_MLA, D_MODEL, D_MODEL)
    idx = [int(v) for v in np.cumsum(sizes)[:-1]]
    return jnp.split(p, idx, axis=-1)


def _axial_rope_tables(n):
    rows = n // GRID_W
    row_pos = jnp.broadcast_to(jnp.arange(rows)[:, None], (rows, GRID_W)).reshape(-1).astype(jnp.float32)
    col_pos = jnp.broadcast_to(jnp.arange(GRID_W)[None, :], (rows, GRID_W)).reshape(-1).astype(jnp.float32)
    axis_dim = QK_ROPE // 2
    inv_freq = ROPE_THETA ** (-jnp.arange(0, axis_dim, 2, dtype=jnp.float32) / axis_dim)
    ang_r = row_pos[:, None] * inv_freq[None, :]
    ang_c = col_pos[:, None] * inv_freq[None, :]
    return (jnp.cos(ang_r), jnp.sin(ang_r), jnp.cos(ang_c), jnp.sin(ang_c))


def _rotate(t, cos, sin):
    n = cos.shape[0]
    shp = (n,) + (1,) * (t.ndim - 3) + (cos.shape[-1],)
    cos = cos.reshape(shp).astype(t.dtype)
    sin = sin.reshape(shp).astype(t.dtype)
    h = t.shape[-1] // 2
    t1, t2 = t[..., :h], t[..., h:]
    return jnp.concatenate([t1 * cos - t2 * sin, t2 * cos + t1 * sin], axis=-1)


def _axial_rope(t, tables):
    cr, sr, cc, sc = tables
    h = t.shape[-1] // 2
    return jnp.concatenate([_rotate(t[..., :h], cr, sr), _rotate(t[..., h:], cc, sc)], axis=-1)


def _short_conv(u, w):
    n = u.shape[1]
    up = jnp.pad(u, ((0, 0), (1, 1), (0, 0)))
    return up[:, :n] * w[0] + up[:, 1:n + 1] * w[1] + up[:, 2:] * w[2]


def _mla_q(cq, q_norm_g, w_uq):
    q = _rms_norm(cq, q_norm_g) @ w_uq
    q = q.reshape(cq.shape[:-1] + (MLA_HEADS, QK_NOPE + QK_ROPE))
    return q[..., :QK_NOPE], q[..., QK_NOPE:]


def _mla_kv(ckv, kv_norm_g, w_ukv):
    kv = _rms_norm(ckv, kv_norm_g) @ w_ukv
    kv = kv.reshape(ckv.shape[:-1] + (MLA_HEADS, QK_NOPE + V_DIM))
    return kv[..., :QK_NOPE], kv[..., QK_NOPE:]


def _mla_attend(qn, qr, kn, kr, v):
    scale = (QK_NOPE + QK_ROPE) ** -0.5
    s = jnp.einsum('bqhd,bkhd->bhqk', qn, kn) + jnp.einsum('bqhd,bkd->bhqk', qr, kr)
    p = jax.nn.softmax(s.astype(jnp.float32) * scale, axis=-1).astype(v.dtype)
    return jnp.einsum('bhqk,bkhd->bqhd', p, v)


def _mla_attend_blocked(qn, qr, kn, kr, v):
    b, n = qn.shape[:2]
    nb = n // Q_BLOCK

    def to_blocks(t):
        return t.reshape((b, nb, Q_BLOCK) + t.shape[2:]).swapaxes(0, 1)

    out = lax.map(lambda qb: _mla_attend(qb[0], qb[1], kn, kr, v), (to_blocks(qn), to_blocks(qr)))
    return out.swapaxes(0, 1).reshape(b, n, MLA_HEADS, V_DIM)


def _mixer_out(parts, att, conv_w, w_out_conv, w_out_mla, w_o):
    xc, bc, cc, gc, _, _, _, gm, g_conv, g_mla = parts
    y_conv = (jax.nn.silu(gc) * bc * _short_conv(cc * xc, conv_w)) @ w_out_conv
    y_mla = (jax.nn.silu(gm) * att.reshape(att.shape[:2] + (D_MLA,))) @ w_out_mla
    merged = jax.nn.sigmoid(g_conv) * y_conv + jax.nn.sigmoid(g_mla) * y_mla
    return merged @ w_o


def _hybrid_layer(x, ctx, c_silu, cctx_silu, rope_tables, w_ada, b_ada, w_in, conv_w, q_norm_g, w_uq,
                  kv_norm_g, w_ukv, w_out_conv, w_out_mla, w_o, ln_g, ln_b, update_ctx):
    shift_x, scale_x, gate_x = jnp.split(c_silu @ w_ada + b_ada, 3, axis=-1)
    shift_c, scale_c, gate_c = jnp.split(cctx_silu @ w_ada + b_ada, 3, axis=-1)
    hx = x * (1 + scale_x[:, None, :]) + shift_x[:, None, :]
    hc = ctx * (1 + scale_c) + shift_c
    px = _split_proj(hx @ w_in)
    pc = _split_proj(hc @ w_in)

    kn_c, v_c = _mla_kv(pc[5], kv_norm_g, w_ukv)
    kr_c = pc[6]
    kn_x, v_x = _mla_kv(px[5], kv_norm_g, w_ukv)
    kr_x = _axial_rope(px[6], rope_tables)
    qn_x, qr_x = _mla_q(px[4], q_norm_g, w_uq)
    qr_x = _axial_rope(qr_x, rope_tables)
    kn_all = jnp.concatenate([kn_c, kn_x], axis=1)
    kr_all = jnp.concatenate([kr_c, kr_x], axis=1)
    v_all = jnp.concatenate([v_c, v_x], axis=1)
    att_x = _mla_attend_blocked(qn_x, qr_x, kn_all, kr_all, v_all)
    y_x = _mixer_out(px, att_x, conv_w, w_out_conv, w_out_mla, w_o)
    x_new = _layer_norm(DEEPNORM_ALPHA * x + gate_x[:, None, :] * y_x, ln_g, ln_b)

    if update_ctx:
        qn_c, qr_c = _mla_q(pc[4], q_norm_g, w_uq)
        att_c = _mla_attend(qn_c, qr_c, kn_c, kr_c, v_c)
        y_c = _mixer_out(pc, att_c, conv_w, w_out_conv, w_out_mla, w_o)
        ctx = _layer_norm(DEEPNORM_ALPHA * ctx + gate_c * y_c, ln_g, ln_b)
    return x_new, ctx


def setup_inputs(seed: int = 0) -> dict:
    key = jax.random.key(seed)
    ks = jax.random.split(key, 20)
    f32 = jnp.float32

    def nrm(k, shape, s):
        return jax.random.normal(k, shape, f32) * s

    L = DEPTH
    return {
        "x": nrm(ks[0], (BATCH, SEQ, D_MODEL), 1.0),
        "c": nrm(ks[1], (BATCH, D_MODEL), 1.0),
        "ctx": nrm(ks[2], (BATCH, CTX_LEN, D_MODEL), 1.0),
        "c_ctx": nrm(ks[3], (D_MODEL,), 1.0),
        "w_ada": nrm(ks[4], (L, D_MODEL, 3 * D_MODEL), 0.5 * D_MODEL ** -0.5),
        "b_ada": nrm(ks[5], (L, 3 * D_MODEL), 0.01),
        "w_in": nrm(ks[6], (L, D_MODEL, N_IN), D_MODEL ** -0.5),
        "conv_w": nrm(ks[7], (L, CONV_K, D_CONV), CONV_K ** -0.5),
        "q_norm_g": 1.0 + nrm(ks[8], (L, Q_LORA), 0.02),
        "w_uq": nrm(ks[9], (L, Q_LORA, MLA_HEADS * (QK_NOPE + QK_ROPE)), Q_LORA ** -0.5),
        "kv_norm_g": 1.0 + nrm(ks[10], (L, KV_LORA), 0.02),
        "w_ukv": nrm(ks[11], (L, KV_LORA, MLA_HEADS * (QK_NOPE + V_DIM)), KV_LORA ** -0.5),
        "w_out_conv": nrm(ks[12], (L, D_CONV, D_MODEL), DEEPNORM_BETA * D_CONV ** -0.5),
        "w_out_mla": nrm(ks[13], (L, D_MLA, D_MODEL), DEEPNORM_BETA * D_MLA ** -0.5),
        "w_o": nrm(ks[14], (L, D_MODEL, D_MODEL), DEEPNORM_BETA * D_MODEL ** -0.5),
        "ln_g": 1.0 + nrm(ks[15], (L, D_MODEL), 0.02),
        "ln_b": nrm(ks[16], (L, D_MODEL), 0.01),
    }


def reference(x, c, ctx, c_ctx, w_ada, b_ada, w_in, conv_w, q_norm_g, w_uq, kv_norm_g, w_ukv,
              w_out_conv, w_out_mla, w_o, ln_g, ln_b):
    rope_tables = _axial_rope_tables(x.shape[1])
    c_silu = jax.nn.silu(c)
    cctx_silu = jax.nn.silu(c_ctx)
    for l in range(DEPTH):
        x, ctx = _hybrid_layer(x, ctx, c_silu, cctx_silu, rope_tables, w_ada[l], b_ada[l], w_in[l], conv_w[l],
                               q_norm_g[l], w_uq[l], kv_norm_g[l], w_ukv[l], w_out_conv[l], w_out_mla[l],
                               w_o[l], ln_g[l], ln_b[l], l < DEPTH - 1)
    return x
```

```python
import numpy as np
import concourse.bass as bass
import concourse.mybir as mybir
from concourse.bass_utils import run_bass_kernel_spmd

F32 = mybir.dt.float32
BF16 = mybir.dt.bfloat16
AF = mybir.ActivationFunctionType
ALU = mybir.AluOpType

D = 1024
NTOK = 2048
NCTX = 256
NKEY = NTOK + NCTX
NIN = 5024
LN_EPS = 1e-5
RMS_EPS = 1e-6
ALPHA = 2.0 ** 0.25
SM_SCALE = 96.0 ** -0.5
ND = 32


class Op:
    __slots__ = ("eng", "dma", "sem", "val", "idx", "key")


class Prog:
    ENGS = ("sp", "act", "pool", "pe", "dve")

    def __init__(self, nc, sems, dma_sems):
        self.nc = nc
        self.eobj = {"sp": nc.sync, "act": nc.scalar, "pool": nc.gpsimd, "pe": nc.tensor, "dve": nc.vector}
        self.sems = sems
        self.dma_sems = dma_sems
        self.cnt = {e: 0 for e in self.ENGS}
        self.nops = {e: 0 for e in self.ENGS}
        self.known = {e: {} for e in self.ENGS}
        self.lastw = {}
        self.readers = {}
        self.barrier_op = None
        self.dmas = []
        self.pending_dmas = []
        self.last_op = {}

    def _needs_wait(self, eng, is_dma, idx, d):
        if d.dma:
            return True
        if d.eng != eng:
            return True
        if is_dma:
            return True
        if eng == "pe":
            return False
        return (idx - d.idx) <= 2

    def add(self, eng, fn, reads=(), writes=(), dma=False, deps=()):
        deps = set(deps)
        if self.barrier_op is not None and eng != "pe":
            deps.add(self.barrier_op)
        for r in reads:
            deps |= self.lastw.get(r, set())
        for w in writes:
            deps |= self.lastw.get(w, set())
            deps |= self.readers.get(w, set())
        op = Op()
        op.eng = eng
        op.dma = dma
        op.idx = self.nops[eng]
        self.nops[eng] += 1
        if dma:
            i = len(self.dmas)
            if i >= ND:
                deps.add(self.dmas[i - ND])
            op.sem = self.dma_sems[i % ND]
            op.val = 16 * (i // ND + 1)
            op.key = ("d", i % ND)
            self.dmas.append(op)
            self.pending_dmas.append(op)
        else:
            op.sem = self.sems[eng]
            op.key = ("e", eng)
            op.val = None
        e = self.eobj[eng]
        need = {}
        for d in deps:
            if not self._needs_wait(eng, dma, op.idx, d):
                continue
            if need.get(d.key, (None, 0))[1] < d.val:
                need[d.key] = (d.sem, d.val)
        kn = self.known[eng]
        for k, (sem, val) in need.items():
            if kn.get(k, 0) < val:
                e.wait_ge(sem, val)
                kn[k] = val
        if fn is not None:
            ins = fn(e)
            if dma:
                ins.then_inc(op.sem, 16)
            else:
                self.cnt[eng] += 1
                op.val = self.cnt[eng]
                ins.then_inc(op.sem, 1)
        else:
            op.val = self.cnt[eng]
        for r in reads:
            self.readers.setdefault(r, set()).add(op)
        for w in writes:
            if self.readers.get(w):
                self.lastw[w] = {op}
                self.readers[w] = set()
            else:
                self.lastw.setdefault(w, set()).add(op)
        self.last_op[eng] = op
        return op

    def barrier(self, scratch_ap):
        deps = set(self.last_op.values()) | set(self.pending_dmas)
        b = self.add("dve", lambda e: e.memset(scratch_ap, 0.0), deps=deps)
        self.barrier_op = b
        self.pending_dmas = []
        return b


class Ring:
    def __init__(self, items):
        self.items = items
        self.i = 0

    def next(self):
        it = self.items[self.i % len(self.items)]
        self.i += 1
        return it


def build_program(debug=()):
    nc = bass.Bass("TRN2", target_bir_lowering=False)

    def din(name, shape):
        return nc.dram_tensor(name, list(shape), F32, kind="ExternalInput").ap()

    x_d = din("x", [NTOK, D])
    ctx_d = din("ctx", [NCTX, D])
    cc_d = din("cc", [128, 16])
    wada_d = din("w_ada", [D, 3 * D])
    bada_d = din("bada", [128, 32])
    badag_d = din("badag", [D])
    win_d = din("w_in", [D, NIN])
    wkr_d = din("w_kr", [D, 256])
    convw_d = din("convw", [128, 12])
    gq_d = din("gq", [128, 2])
    gkv_d = din("gkv", [128, 1])
    wuq_d = din("w_uq", [256, 1024])
    wukv_d = din("w_ukv", [128, 1024])
    woc_d = din("w_oc", [512, D])
    wom_d = din("w_om", [512, D])
    wo_d = din("w_o", [D, D])
    lng_d = din("ln_g", [D])
    lnb_d = din("ln_b", [D])
    tabq_d = din("tabq", [128, NTOK])
    tabk_d = din("tabk", [2, 64, NKEY])
    out_d = nc.dram_tensor("out", [NTOK, D], F32, kind="ExternalOutput").ap()

    dbg_outs = {}

    from contextlib import ExitStack

    with ExitStack() as top:
        sems = {e: top.enter_context(nc.semaphore("s_" + e)) for e in Prog.ENGS}
        dma_sems = [top.enter_context(nc.semaphore("d%d" % i)) for i in range(ND)]
        P = Prog(nc, sems, dma_sems)
        A = P.add

        def sb(stack, name, shape, dt):
            return stack.enter_context(nc.sbuf_tensor(name, list(shape), dt))

        def ps(stack, name, shape, dt=F32):
            return stack.enter_context(nc.psum_tensor(name, list(shape), dt))

        def dma(out, in_):
            return lambda e: e.dma_start(out=out, in_=in_)

        def dump(name, ap, shape, dt, reads):
            if name not in debug:
                return
            t = nc.dram_tensor("dbg_" + name, list(shape), dt, kind="ExternalOutput").ap()
            dbg_outs[name] = A("sp", dma(t, ap), reads=reads, dma=True)

        hxT = sb(top, "hxT", [128, 8, NTOK], BF16)
        hcT = sb(top, "hcT", [128, 8, NCTX], BF16)
        attT = sb(top, "attT", [128, 4, NTOK], BF16)
        ident = sb(top, "ident", [128, 128], BF16)
        ones_bf = sb(top, "ones_bf", [128, 128], BF16)
        cc_sb = sb(top, "cc_sb", [128, 16], F32)
        cs = sb(top, "cs", [128, 16], F32)
        mod = sb(top, "mod", [128, 32], F32)
        bada_sb = sb(top, "bada_sb", [128, 32], F32)
        consts = sb(top, "consts", [128, 8], F32)
        scratch = sb(top, "scratch", [128, 8], F32)
        convw_sb = sb(top, "convw_sb", [128, 12], F32)
        gq_sb = sb(top, "gq_sb", [128, 2], F32)
        gkv_sb = sb(top, "gkv_sb", [128, 1], F32)
        wst = [sb(top, "wst%d" % i, [128, 8, 128], F32) for i in range(3)]
        wbf = [sb(top, "wbf%d" % i, [128, 8, 128], BF16) for i in range(5)]
        wst_r = Ring([0, 1, 2])
        wbf_r = Ring([0, 1, 2, 3, 4])
        gate_bc = sb(top, "gate_bc", [128, D], F32)
        pb_t = [ps(top, "pb%d" % i, [128, 1024]) for i in range(4)]

        def bank_f32(i):
            return pb_t[i // 2][:, (i % 2) * 512:(i % 2) * 512 + 512]

        def bank_bf16(i):
            return bank_f32(i).bitcast(BF16)

        all_banks = [(bank_f32(i), ("bank", i)) for i in range(8)]

        A("pool", lambda e: e.memset(ident[:], 0.0), writes=["ident"])
        A("pool", lambda e: e.affine_select(out=ident[:], in_=ident[:], pattern=[[-1, 128]],
                                            compare_op=ALU.not_equal, fill=1.0, base=0, channel_multiplier=1),
          reads=["ident"], writes=["ident"])
        A("pool", lambda e: e.memset(ones_bf[:], 1.0), writes=["ones"])
        A("pool", lambda e: e.memset(consts[:, 0:1], RMS_EPS), writes=["consts"])
        A("pool", lambda e: e.memset(consts[:, 1:2], LN_EPS), writes=["consts"])
        A("sp", dma(cc_sb[:], cc_d), writes=["cc"], dma=True)
        A("sp", dma(bada_sb[:], bada_d), writes=["bada"], dma=True)
        A("sp", dma(convw_sb[:], convw_d), writes=["convw"], dma=True)
        A("sp", dma(gq_sb[:], gq_d), writes=["gq"], dma=True)
        A("sp", dma(gkv_sb[:], gkv_d), writes=["gkv"], dma=True)
        A("act", lambda e: e.activation(out=cs[:], in_=cc_sb[:], func=AF.Silu), reads=["cc"], writes=["cs"])

        wsrcs = [win_d[:, 2304:2432], wkr_d[:, 0:128], wkr_d[:, 128:256], win_d[:, 2048:2176], win_d[:, 2176:2304]]
        wsrcs += [win_d[:, 2464 + c * 128:2464 + (c + 1) * 128] for c in range(4)]
        for j in range(4):
            wsrcs += [win_d[:, o + j * 128:o + (j + 1) * 128] for o in (0, 1024, 512, 1536)]
        for i in range(8):
            wsrcs += [win_d[:, o + i * 128:o + (i + 1) * 128] for o in (2976, 4000)]
        ws_state = {"issued": 0, "taken": 0, "released": 0, "slots": [], "cast": "act", "hook": None}
        AHEAD = 3

        def _issue_w():
            k = ws_state["issued"]
            src = wsrcs[k]
            ws_state["issued"] += 1
            s = wst_r.next()
            b = wbf_r.next()
            A("sp", dma(wst[s][:], src.rearrange("(k p) m -> p k m", p=128)), writes=[("wst", s)], dma=True)
            ce = ws_state["cast"]
            if ce == "act":
                A("act", lambda e: e.copy(out=wbf[b][:], in_=wst[s][:]), reads=[("wst", s)], writes=[("wbf", b)])
            else:
                A(ce, lambda e: e.tensor_copy(out=wbf[b][:], in_=wst[s][:]), reads=[("wst", s)], writes=[("wbf", b)])
            ws_state["slots"].append(b)
            if ws_state["hook"] is not None:
                ws_state["hook"]()

        def ws_prefetch(n_ahead=AHEAD):
            while (ws_state["issued"] < len(wsrcs) and ws_state["issued"] - ws_state["taken"] < n_ahead
                   and ws_state["issued"] - ws_state["released"] < len(wbf)):
                _issue_w()

        def next_w():
            if ws_state["issued"] == ws_state["taken"]:
                assert ws_state["issued"] - ws_state["released"] < len(wbf)
                _issue_w()
            b = ws_state["slots"].pop(0)
            ws_state["taken"] += 1
            ws_prefetch()
            return b

        def rel_w(n=1):
            ws_state["released"] += n
            ws_prefetch()

        def px_mm(b, n, ps_ap, ps_res, is_ctx=False, m0=0, m1=128):
            if is_ctx:
                rhs = [hcT[:, kc, :] for kc in range(8)]
                reads = [("hcT", kc) for kc in range(8)]
            else:
                rhs = [hxT[:, kc, n * 512:(n + 1) * 512] for kc in range(8)]
                reads = [("hxT", kc, n // 2) for kc in range(8)]

            def fn(e):
                for kc in range(8):
                    ins = e.matmul(ps_ap, lhsT=wbf[b][:, kc, m0:m1], rhs=rhs[kc], start=(kc == 0), stop=(kc == 7))
                return ins

            return A("pe", fn, reads=reads + [("wbf", b)], writes=[ps_res])

        with ExitStack() as s0:
            NXS = 16
            wa = [sb(s0, "wa%d" % i, [128, 2048], F32) for i in range(2)]
            wab = [sb(s0, "wab%d" % i, [128, 2048], BF16) for i in range(2)]
            xbf = [sb(s0, "xbf%d" % i, [128, D], BF16) for i in range(NXS)]
            cs_bf = sb(s0, "cs_bf", [128, 16], BF16)
            modps = bank_f32(7)

            A("dve", lambda e: e.tensor_copy(out=cs_bf[:], in_=cs[:]), reads=["cs"], writes=["cs_bf"])

            EVQ = {0: (0, 0), 2: (0, 1), 4: (0, 2), 1: (1, 0), 3: (1, 1), 5: (1, 2), 6: (1, 3), 7: (1, 4)}
            tiles = [(x_d[j * 128:(j + 1) * 128, :], hxT, j * 128, 0) for j in range(16)]
            tiles += [(ctx_d[t * 128:(t + 1) * 128, :], hcT, t * 128, 1) for t in range(2)]
            NT = len(tiles)

            x_ops = []

            def load_tile(j):
                s = j % NXS
                dd = [wa_ops[min(4 + j // 4, 7)]]
                if j >= 2:
                    dd.append(x_ops[j - 2])
                x_ops.append(A("pool", dma(xbf[s][:], tiles[j][0]), writes=[("xbf", s)], dma=True, deps=dd))

            def cast_tile(j):
                return

            groups = [list(range(4 * g, 4 * g + 4)) for g in range(4)] + [[16, 17]]
            tpb = Ring([(bank_bf16(i), ("bank", i)) for i in range(6)])

            def do_group(g):
                tl = groups[g]
                _, dstT, c0, mj = tiles[tl[0]]
                W = 128 * len(tl)
                for kc in range(8):
                    bank, bres = tpb.next()

                    def fn(e, bank=bank, kc=kc):
                        for i, j in enumerate(tl):
                            ins = e.transpose(bank[:, i * 128:(i + 1) * 128],
                                              xbf[j % NXS][:, kc * 128:(kc + 1) * 128], ident[:])
                        return ins

                    A("pe", fn, reads=[("xbf", j % NXS) for j in tl] + ["ident"], writes=[bres])
                    dst = dstT[:, kc, c0:c0 + W]
                    sc_ap = mod[:, 16 + 2 * kc + mj:16 + 2 * kc + mj + 1]
                    sh_ap = mod[:, 2 * kc + mj:2 * kc + mj + 1]
                    res = ("hxT", kc, c0 // 1024) if mj == 0 else ("hcT", kc)
                    if kc % 2 == 0:
                        A("act", lambda e, dst=dst, bank=bank, sc_ap=sc_ap, sh_ap=sh_ap: e.activation(
                            out=dst, in_=bank[:, 0:W], func=AF.Identity, bias=sh_ap, scale=sc_ap),
                          reads=[bres, "mod"], writes=[res])
                    else:
                        A("dve", lambda e, dst=dst, bank=bank, sc_ap=sc_ap, sh_ap=sh_ap: e.tensor_scalar(
                            out=dst, in0=bank[:, 0:W], scalar1=sc_ap, scalar2=sh_ap, op0=ALU.mult, op1=ALU.add),
                          reads=[bres, "mod"], writes=[res])

            wa_ops = []
            for kc in range(8):
                s = kc % 2
                wa_ops.append(A("sp", dma(wa[s][:], wada_d[kc * 128:(kc + 1) * 128, 0:2048]), writes=[("wa", s)],
                                dma=True))
                A("dve", lambda e, s=s: e.tensor_copy(out=wab[s][:], in_=wa[s][:]), reads=[("wa", s)],
                  writes=[("wab", s)])

                def fn(e, kc=kc, s=s):
                    for m in range(16):
                        ins = e.matmul(modps[:, 2 * m:2 * m + 2], lhsT=wab[s][:, m * 128:(m + 1) * 128],
                                       rhs=cs_bf[:, 2 * kc:2 * kc + 2], start=(kc == 0 and m == 0), stop=(kc == 7),
                                       skip_group_check=True)
                    return ins

                A("pe", fn, reads=[("wab", s), "cs_bf"], writes=[("bank", 7)])
            for j in range(16):
                load_tile(j)
                if j == 3:
                    ws_prefetch()
            A("dve", lambda e: e.tensor_tensor(out=mod[:], in0=modps[:, 0:32], in1=bada_sb[:], op=ALU.add),
              reads=[("bank", 7), "bada"], writes=["mod"])
            A("dve", lambda e: e.tensor_scalar_add(out=mod[:, 16:32], in0=mod[:, 16:32], scalar1=1.0),
              reads=["mod"], writes=["mod"])
            for g in range(len(groups)):
                do_group(g)
                if g == 0:
                    load_tile(16)
                    load_tile(17)
            if "stopA" in debug:
                dump("hxT", hxT[:], [128, 8, NTOK], BF16, [("hxT", kc, h) for kc in range(8) for h in range(2)])
                A("sp", None, deps=list(dbg_outs.values()))
                return nc
            dump("hxT", hxT[:], [128, 8, NTOK], BF16, [("hxT", kc, h) for kc in range(8) for h in range(2)])
            dump("hcT", hcT[:], [128, 8, NCTX], BF16, [("hcT", kc) for kc in range(8)])
            dump("mod", mod[:], [128, 32], F32, ["mod"])
            P.barrier(scratch[:, 0:1])

        with ExitStack() as s1:
            kT = [sb(s1, "kT%d" % h, [128, NKEY], BF16) for h in range(8)]
            Vaug = sb(s1, "Vaug", [128, 18, 768], BF16)
            kchunks = [(0, 256, True, 0)] + [(256 + 512 * n, 512, False, n) for n in range(4)]

            with ExitStack() as skv:
                ckvnT = sb(skv, "ckvnT", [128, NKEY], BF16)
                krot = sb(skv, "krot", [128, NKEY], BF16)
                Ck = sb(skv, "Ck", [128, NKEY], F32)
                Sk = sb(skv, "Sk", [128, NKEY], F32)
                t1 = [sb(skv, "t1_%d" % i, [128, 512], F32) for i in range(2)]
                t2 = [sb(skv, "t2_%d" % i, [128, 512], F32) for i in range(2)]
                sq = [sb(skv, "sq%d" % i, [128, 512], BF16) for i in range(2)]
                rs = [sb(skv, "rs%d" % i, [128, 512], F32) for i in range(2)]
                wukv_st = sb(skv, "wukv_st", [128, 1024], F32)
                wukv_bf = sb(skv, "wukv_bf", [128, 1024], BF16)
                banks = Ring(list(all_banks))

                ws_state["cast"] = "pool"
                b = next_w()
                bA = next_w()
                bB = next_w()
                A("sp", dma(Ck[64:128, :], tabk_d[0]), writes=["Ck"], dma=True)
                A("sp", dma(Sk[64:128, :], tabk_d[1]), writes=["Sk"], dma=True)
                A("sp", dma(wukv_st[:], wukv_d), writes=["wukv_st"], dma=True)

                csrep = sb(skv, "csrep", [128, 8, 128], F32)
                wag = [sb(skv, "wag%d" % i, [128, D], F32) for i in range(2)]
                bgate = sb(skv, "bgate", [128, D], F32)
                A("sp", dma(bgate[:], badag_d.partition_broadcast(128)), writes=["bgate"], dma=True)
                A("pool", lambda e: e.memset(csrep[:], 1.0), writes=[("csrep", kc) for kc in range(8)])
                def gate_prep_dve():
                    for kc in range(8):
                        A("dve", lambda e, kc=kc: e.tensor_scalar_mul(out=csrep[:, kc, :], in0=csrep[:, kc, :],
                                                                      scalar1=cs[:, 2 * kc:2 * kc + 1]),
                          reads=["cs", ("csrep", kc)], writes=[("csrep", kc)])

                allb = banks.items
                ring6 = Ring(allb[0:6])
                gpa = allb[6:8]

                def gate_dma(kc):
                    s = kc % 2
                    A("sp", dma(wag[s][:], wada_d[kc * 128:(kc + 1) * 128, 2048:3072]), writes=[("wag", s)], dma=True)

                def gate_step(kc):
                    s = kc % 2

                    def fn(e):
                        for hf in range(2):
                            ins = e.matmul(gpa[hf][0], lhsT=csrep[:, kc, :], rhs=wag[s][:, hf * 512:(hf + 1) * 512],
                                           start=(kc == 0), stop=(kc == 7))
                        return ins

                    A("pe", fn, reads=[("csrep", kc), ("wag", s)], writes=[gpa[0][1], gpa[1][1]])
                    if kc + 2 < 8:
                        gate_dma(kc + 2)

                gate_dma(0)
                gate_dma(1)
                gate_sched = {0: [], 1: [0, 1], 2: [2, 3], 3: [4, 5], 4: [6, 7]}

                for g in range(6):
                    A("pool", lambda e, g=g: e.memset(
                        Vaug[:, 3 * g:3 * g + 3, :].rearrange("p k (c s d) -> p k c s d", c=4, s=3)[:, :, :, 1, :], 1.0),
                      writes=[("V", kt) for kt in range(3 * g, 3 * g + 3)])
                for ci, (k0, W, isc, n) in enumerate(kchunks):
                    s = ci % 2
                    pa, pr = ring6.next()
                    px_mm(b, n, pa[:, :W], pr, is_ctx=isc)
                    A("act", lambda e, pa=pa, s=s, W=W: e.activation(out=sq[s][:, :W], in_=pa[:, :W], func=AF.Square),
                      reads=[pr], writes=[("sq", s)])
                    paA, prA = ring6.next()
                    px_mm(bA, n, paA[:, :W], prA, is_ctx=isc)
                    paB, prB = ring6.next()
                    px_mm(bB, n, paB[:, :W], prB, is_ctx=isc)
                    pa2, pr2 = ring6.next()
                    A("pe", lambda e, pa2=pa2, s=s, W=W: e.matmul(pa2[:, :W], lhsT=ones_bf[:], rhs=sq[s][:, :W],
                                                                  start=True, stop=True),
                      reads=[("sq", s), "ones"], writes=[pr2])
                    A("act", lambda e, pa2=pa2, s=s, W=W: e.activation(out=rs[s][:, :W], in_=pa2[:, :W], func=AF.Ln,
                                                                       bias=consts[:, 0:1], scale=1.0 / 128.0),
                      reads=[pr2, "consts"], writes=[("rs", s)])
                    A("act", lambda e, s=s, W=W: e.activation(out=rs[s][:, :W], in_=rs[s][:, :W], func=AF.Exp,
                                                              scale=-0.5),
                      reads=[("rs", s)], writes=[("rs", s)])
                    A("dve", lambda e, paA=paA, s=s, W=W, k0=k0: e.tensor_tensor(
                        out=t1[s][64:128, :W], in0=paA[64:128, :W], in1=Ck[64:128, k0:k0 + W], op=ALU.mult),
                      reads=[prA, "Ck"], writes=[("t1", s)])
                    A("dve", lambda e, paB=paB, s=s, W=W, k0=k0: e.tensor_tensor(
                        out=t2[s][64:128, :W], in0=paB[64:128, :W], in1=Sk[64:128, k0:k0 + W], op=ALU.mult),
                      reads=[prB, "Sk"], writes=[("t2", s)])
                    A("dve", lambda e, s=s, W=W, k0=k0: e.tensor_tensor(
                        out=krot[64:128, k0:k0 + W], in0=t1[s][64:128, :W], in1=t2[s][64:128, :W], op=ALU.add),
                      reads=[("t1", s), ("t2", s)], writes=[("krot", ci)])
                    A("dve", lambda e, pa=pa, s=s, W=W, k0=k0: e.tensor_tensor(
                        out=ckvnT[:, k0:k0 + W], in0=pa[:, :W], in1=rs[s][:, :W], op=ALU.mult),
                      reads=[pr, ("rs", s)], writes=[("ckvn", ci)])
                    if ci == 0:
                        gate_prep_dve()
                    for kc in gate_sched[ci]:
                        gate_step(kc)
                rel_w(3)
                A("dve", lambda e: e.tensor_scalar_mul(out=wukv_bf[:], in0=wukv_st[:], scalar1=gkv_sb[:, 0:1]),
                  reads=["wukv_st", "gkv"], writes=["wukv_bf"])
                for h in range(8):
                    A("sp", dma(kT[h][64:128, :], krot[64:128, :]), reads=[("krot", ci) for ci in range(5)],
                      writes=[("kT", h, ci, 1) for ci in range(5)], dma=True)
                for hf in range(2):
                    A("dve", lambda e, hf=hf: e.tensor_tensor(out=gate_bc[:, hf * 512:(hf + 1) * 512], in0=gpa[hf][0],
                                                              in1=bgate[:, hf * 512:(hf + 1) * 512], op=ALU.add),
                      reads=[gpa[hf][1], "bgate"], writes=[("gate", hf)])
                dump("gate", gate_bc[:], [128, D], F32, [("gate", 0), ("gate", 1)])
                for h in range(8):
                    for ci, (k0, W, isc, n) in enumerate(kchunks):
                        pa, pr = ring6.next()
                        A("pe", lambda e, pa=pa, h=h, W=W, k0=k0: e.matmul(
                            pa[0:64, :W], lhsT=wukv_bf[:, h * 64:(h + 1) * 64], rhs=ckvnT[:, k0:k0 + W],
                            start=True, stop=True), reads=["wukv_bf", ("ckvn", ci)], writes=[pr])
                        if (h + ci) % 2 == 0:
                            A("act", lambda e, pa=pa, h=h, W=W, k0=k0: e.copy(out=kT[h][0:64, k0:k0 + W],
                                                                              in_=pa[0:64, :W]),
                              reads=[pr], writes=[("kT", h, ci, 0)])
                        else:
                            A("dve", lambda e, pa=pa, h=h, W=W, k0=k0: e.tensor_copy(out=kT[h][0:64, k0:k0 + W],
                                                                                     in_=pa[0:64, :W]),
                              reads=[pr], writes=[("kT", h, ci, 0)])
                for kt in range(18):
                    ci = 0 if kt < 2 else 1 + (kt - 2) // 4
                    pa, pr = ring6.next()
                    A("pe", lambda e, pa=pa, kt=kt: e.matmul(pa[:, 0:512], lhsT=ckvnT[:, kt * 128:(kt + 1) * 128],
                                                             rhs=wukv_bf[:, 512:1024], start=True, stop=True),
                      reads=["wukv_bf", ("ckvn", ci)], writes=[pr])
                    dst = Vaug[:, kt, :].rearrange("p (c s d) -> p c s d", c=4, s=3)[:, :, 0:3:2, :]
                    src = pa[:, 0:512].rearrange("p (c t d) -> p c t d", c=4, t=2)
                    if kt % 2 == 0:
                        A("act", lambda e, dst=dst, src=src: e.copy(out=dst, in_=src), reads=[pr], writes=[("V", kt)])
                    else:
                        A("dve", lambda e, dst=dst, src=src: e.tensor_copy(out=dst, in_=src), reads=[pr],
                          writes=[("V", kt)])

                dump("ckvnT", ckvnT[:], [128, NKEY], BF16, [("ckvn", ci) for ci in range(5)])
                dump("kT0", kT[0][:], [128, NKEY], BF16, [("kT", 0, ci, j) for ci in range(5) for j in range(2)])
                dump("kT3", kT[3][:], [128, NKEY], BF16, [("kT", 3, ci, j) for ci in range(5) for j in range(2)])
                dump("Vaug", Vaug[:], [128, 18, 768], BF16, [("V", kt) for kt in range(18)])
                if "stopKV" in debug:
                    A("sp", None, deps=list(dbg_outs.values()) + list(P.last_op.values()) + P.pending_dmas)
                    return nc
                P.barrier(scratch[:, 0:1])

            qT = [sb(s1, "qT%d" % h, [128, NTOK], BF16) for h in range(8)]
            with ExitStack() as sq_:
                cqnT = sb(sq_, "cqnT", [128, 2, NTOK], BF16)
                Tq = sb(sq_, "Tq", [128, NTOK], F32)
                wuq_st = sb(sq_, "wuq_st", [128, 1024], F32)
                wuq_bf = sb(sq_, "wuq_bf", [128, 2, 1024], BF16)
                sq = [sb(sq_, "sqq%d" % i, [128, 512], BF16) for i in range(2)]
                rs = [sb(sq_, "rsq%d" % i, [128, 512], F32) for i in range(2)]
                banks = Ring(list(all_banks))

                A("sp", dma(Tq[:], tabq_d), writes=["Tq"], dma=True)
                for k in range(2):
                    A("sp", dma(wuq_st[:], wuq_d[k * 128:(k + 1) * 128, :]), writes=["wuq_st"], dma=True)
                    A("dve", lambda e, k=k: e.tensor_scalar_mul(out=wuq_bf[:, k, :], in0=wuq_st[:],
                                                                scalar1=gq_sb[:, k:k + 1]),
                      reads=["wuq_st", "gq"], writes=[("wuq_bf", k)])
                b0 = next_w()
                b1 = next_w()
                def q_finish(n, pa0, pr0, pa1, pr1):
                    s0_, s1_ = 0, 1
                    pa2, pr2 = banks.next()

                    def fn(e):
                        e.matmul(pa2, lhsT=ones_bf[:], rhs=sq[s0_][:], start=True, stop=False)
                        return e.matmul(pa2, lhsT=ones_bf[:], rhs=sq[s1_][:], start=False, stop=True)

                    A("pe", fn, reads=[("sq", s0_), ("sq", s1_), "ones"], writes=[pr2])
                    r = n % 2
                    A("act", lambda e: e.activation(out=rs[r][:], in_=pa2, func=AF.Ln,
                                                    bias=consts[:, 0:1], scale=1.0 / 256.0),
                      reads=[pr2, "consts"], writes=[("rs", r)])
                    A("act", lambda e: e.activation(out=rs[r][:], in_=rs[r][:], func=AF.Exp, scale=-0.5),
                      reads=[("rs", r)], writes=[("rs", r)])
                    A("dve", lambda e: e.tensor_tensor(
                        out=cqnT[:, 0, n * 512:(n + 1) * 512], in0=pa0, in1=rs[r][:], op=ALU.mult),
                      reads=[pr0, ("rs", r)], writes=[("cqn", 0, n)])
                    A("dve", lambda e: e.tensor_tensor(
                        out=cqnT[:, 1, n * 512:(n + 1) * 512], in0=pa1, in1=rs[r][:], op=ALU.mult),
                      reads=[pr1, ("rs", r)], writes=[("cqn", 1, n)])

                pend = None
                for n in range(4):
                    pa0, pr0 = banks.next()
                    px_mm(b0, n, pa0, pr0)
                    pa1, pr1 = banks.next()
                    px_mm(b1, n, pa1, pr1)
                    if pend is not None:
                        q_finish(*pend)
                    A("act", lambda e, pa0=pa0: e.activation(out=sq[0][:], in_=pa0, func=AF.Square),
                      reads=[pr0], writes=[("sq", 0)])
                    A("act", lambda e, pa1=pa1: e.activation(out=sq[1][:], in_=pa1, func=AF.Square),
                      reads=[pr1], writes=[("sq", 1)])
                    pend = (n, pa0, pr0, pa1, pr1)
                q_finish(*pend)
                rel_w(2)
                for h in range(8):
                    for n in range(4):
                        pa, pr = banks.next()

                        def fn(e, pa=pa, h=h, n=n):
                            e.matmul(pa, lhsT=wuq_bf[:, 0, h * 128:(h + 1) * 128], rhs=cqnT[:, 0, n * 512:(n + 1) * 512],
                                     start=True, stop=False)
                            return e.matmul(pa, lhsT=wuq_bf[:, 1, h * 128:(h + 1) * 128],
                                            rhs=cqnT[:, 1, n * 512:(n + 1) * 512], start=False, stop=True)

                        A("pe", fn, reads=[("wuq_bf", 0), ("wuq_bf", 1), ("cqn", 0, n), ("cqn", 1, n)], writes=[pr])
                        A("dve", lambda e, pa=pa, h=h, n=n: e.tensor_tensor(
                            out=qT[h][:, n * 512:(n + 1) * 512], in0=pa, in1=Tq[:, n * 512:(n + 1) * 512], op=ALU.mult),
                          reads=[pr, "Tq"], writes=[("qT", h, n)])
                dump("qT0", qT[0][:], [128, NTOK], BF16, [("qT", 0, n) for n in range(4)])
                dump("qT5", qT[5][:], [128, NTOK], BF16, [("qT", 5, n) for n in range(4)])
                if "stopQ" in debug:
                    A("sp", None, deps=list(dbg_outs.values()) + list(P.last_op.values()) + P.pending_dmas)
                    return nc
                P.barrier(scratch[:, 0:1])

            with ExitStack() as sa:
                NPT = 4
                pt = [sb(sa, "pt%d" % i, [128, 1024], BF16) for i in range(NPT)]
                num = [sb(sa, "num%d" % i, [128, 512], F32) for i in range(2)]
                rden = [sb(sa, "rden%d" % i, [128, 512], F32) for i in range(2)]
                rdsw = [sb(sa, "rdsw%d" % i, [128, 512], F32) for i in range(2)]
                NSP = 3
                Sps = [pb_t[i] for i in range(NSP)]
                accp = [pb_t[3]]

                steps = [(qc, c, hh, ktp) for qc in range(4) for c in range(4) for hh in range(2) for ktp in range(9)]

                def emit_qk(i):
                    qc, c, hh, ktp = steps[i]
                    h = 2 * c + hh
                    s = i % NSP

                    def fn(e):
                        for j in range(2):
                            kt = 2 * ktp + j
                            ins = e.matmul(Sps[s][:, j * 512:(j + 1) * 512], lhsT=kT[h][:, kt * 128:(kt + 1) * 128],
                                           rhs=qT[h][:, qc * 512:(qc + 1) * 512], start=True, stop=True)
                        return ins

                    A("pe", fn, reads=[("kT", h, ci, hf) for ci in range(5) for hf in range(2)] + [("qT", h, qc)],
                      writes=[("bank", 2 * s), ("bank", 2 * s + 1)])

                deferred = []

                def emit_rest(i):
                    qc, c, hh, ktp = steps[i]
                    h = 2 * c + hh
                    s = i % NSP
                    p = i % NPT
                    pair = 0
                    rslot = (qc * 4 + c) % 2
                    A("act", lambda e: e.activation(out=pt[p][:], in_=Sps[s][:], func=AF.Exp, scale=SM_SCALE),
                      reads=[("bank", 2 * s), ("bank", 2 * s + 1)], writes=[("pt", p)])
                    acc = accp[pair][:, hh * 512:(hh + 1) * 512]
                    vbase = c * 192 + hh * 64

                    def fn(e):
                        for j in range(2):
                            kt = 2 * ktp + j
                            ins = e.matmul(acc, lhsT=Vaug[:, kt, vbase:vbase + 128], rhs=pt[p][:, j * 512:(j + 1) * 512],
                                           start=(kt == 0), stop=(kt == 17))
                        return ins

                    A("pe", fn, reads=[("pt", p), ("V", 2 * ktp), ("V", 2 * ktp + 1)], writes=[("bank", 6 + hh)])
                    if ktp == 8:
                        r = rslot
                        nlo, nhi = (0, 64) if hh == 0 else (64, 128)
                        dlo, dhi = (64, 128) if hh == 0 else (0, 64)
                        A("dve", lambda e: e.reciprocal(out=rden[r][dlo:dhi, :], in_=acc[dlo:dhi, :]),
                          reads=[("bank", 6 + hh)], writes=[("rden", r, hh)])
                        A("dve", lambda e: e.tensor_copy(out=num[r][nlo:nhi, :], in_=acc[nlo:nhi, :]),
                          reads=[("bank", 6 + hh)], writes=[("num", r, hh)])
                        A("sp", dma(rdsw[r][nlo:nhi, :], rden[r][dlo:dhi, :]), reads=[("rden", r, hh)],
                          writes=[("rdsw", r, hh)], dma=True)
                        if hh == 1:
                            deferred.append((i + 4, r, c, qc))

                def flush_deferred(i, force=False):
                    while deferred and (force or deferred[0][0] <= i):
                        _, r, c, qc = deferred.pop(0)
                        A("dve", lambda e, r=r, c=c, qc=qc: e.tensor_tensor(
                            out=attT[:, c, qc * 512:(qc + 1) * 512], in0=num[r][:], in1=rdsw[r][:], op=ALU.mult),
                          reads=[("num", r, 0), ("num", r, 1), ("rdsw", r, 0), ("rdsw", r, 1)],
                          writes=[("attT", c, qc)])

                emit_qk(0)
                emit_qk(1)
                for i in range(len(steps)):
                    if i + 2 < len(steps):
                        emit_qk(i + 2)
                    emit_rest(i)
                    flush_deferred(i)
                flush_deferred(0, force=True)
                dump("attT", attT[:], [128, 4, NTOK], BF16, [("attT", c, qc) for c in range(4) for qc in range(4)])
                P.barrier(scratch[:, 0:1])

        with ExitStack() as s2:
            ycv = sb(s2, "ycv", [128, 4, NTOK], BF16)
            merged = sb(s2, "merged", [128, 8, NTOK], BF16)
            woc_bf = sb(s2, "woc_bf", [128, 4, D], BF16)
            wom_bf = sb(s2, "wom_bf", [128, 4, D], BF16)
            wo_bf = sb(s2, "wo_bf", [128, 8, D], BF16)
            stg = [sb(s2, "stg%d" % i, [128, D], F32) for i in range(2)]
            stg_r = Ring([0, 1])
            Sps = [pb_t[i] for i in range(NSP)]
            accp = [pb_t[3]]
            wprep = []
            for (wd, wb, nm) in ((woc_d, woc_bf, "woc"), (wom_d, wom_bf, "wom")):
                for k in range(4):
                    def f(wd=wd, wb=wb, nm=nm, k=k):
                        s = stg_r.next()
                        A("sp", dma(stg[s][:], wd[k * 128:(k + 1) * 128, :]), writes=[("stg", s)], dma=True)
                        A("pool", lambda e: e.tensor_copy(out=wb[:, k, :], in_=stg[s][:]),
                          reads=[("stg", s)], writes=[(nm, k)])
                    wprep.append(f)
            for k in range(8):
                def f(k=k):
                    s = stg_r.next()
                    A("sp", dma(stg[s][:], wo_d[k * 128:(k + 1) * 128, :]), writes=[("stg", s)], dma=True)
                    A("pool", lambda e: e.tensor_tensor(out=wo_bf[:, k, :], in0=stg[s][:], in1=gate_bc[:], op=ALU.mult),
                      reads=[("stg", s)], writes=[("wo", k)])
                wprep.append(f)

            def _hook():
                if wprep:
                    wprep.pop(0)()

            ws_state["hook"] = _hook
            ws_state["cast"] = "act"

            with ExitStack() as sd1:
                sgm = [sb(sd1, "sgm%d" % i, [128, 512], F32) for i in range(2)]
                xcs = sb(sd1, "xcs", [128, NTOK], F32)
                u = sb(sd1, "u", [128, NTOK + 2], F32)
                v = sb(sd1, "v", [128, NTOK], F32)
                A("pool", lambda e: e.memset(u[:, 0:1], 0.0), writes=[("u", "l")])
                A("pool", lambda e: e.memset(u[:, NTOK + 1:NTOK + 2], 0.0), writes=[("u", "r")])
                si = 0
                d1_state = {"held": 0}

                def take1():
                    if d1_state["held"]:
                        rel_w(1)
                    d1_state["held"] = 1
                    return next_w()

                for c in range(4):
                    b = take1()
                    for n in range(4):
                        pa, pr = banks.next()
                        px_mm(b, n, pa, pr)
                        s = si % 2
                        si += 1
                        A("act", lambda e, pa=pa, s=s: e.activation(out=sgm[s][:], in_=pa, func=AF.Silu),
                          reads=[pr], writes=[("sgm", s)])
                        A("dve", lambda e, s=s, c=c, n=n: e.tensor_tensor(
                            out=attT[:, c, n * 512:(n + 1) * 512], in0=attT[:, c, n * 512:(n + 1) * 512], in1=sgm[s][:],
                            op=ALU.mult), reads=[("sgm", s), ("attT", c, n)], writes=[("attT", c, n)])
                for j in range(4):
                    b = take1()
                    for n in range(4):
                        pa, pr = banks.next()
                        px_mm(b, n, pa, pr)
                        A("act", lambda e, pa=pa, n=n: e.copy(out=xcs[:, n * 512:(n + 1) * 512], in_=pa),
                          reads=[pr], writes=[("xcs", n)])
                    b = take1()
                    for n in range(4):
                        pa, pr = banks.next()
                        px_mm(b, n, pa, pr)
                        A("dve", lambda e, pa=pa, n=n: e.tensor_tensor(
                            out=u[:, 1 + n * 512:1 + (n + 1) * 512], in0=pa, in1=xcs[:, n * 512:(n + 1) * 512],
                            op=ALU.mult), reads=[pr, ("xcs", n)], writes=[("u", n)])
                    ures = [("u", n) for n in range(4)] + [("u", "l"), ("u", "r")]
                    A("dve", lambda e, j=j: e.tensor_scalar_mul(out=v[:], in0=u[:, 0:NTOK],
                                                                scalar1=convw_sb[:, 3 * j:3 * j + 1]), reads=ures + ["convw"], writes=["v"])
                    A("dve", lambda e, j=j: e.scalar_tensor_tensor(out=v[:], in0=u[:, 1:NTOK + 1],
                                                                   scalar=convw_sb[:, 3 * j + 1:3 * j + 2], in1=v[:],
                                                                   op0=ALU.mult, op1=ALU.add),
                      reads=ures + ["convw", "v"], writes=["v"])
                    A("dve", lambda e, j=j: e.scalar_tensor_tensor(out=v[:], in0=u[:, 2:NTOK + 2],
                                                                   scalar=convw_sb[:, 3 * j + 2:3 * j + 3], in1=v[:],
                                                                   op0=ALU.mult, op1=ALU.add),
                      reads=ures + ["convw", "v"], writes=["v"])
                    b = take1()
                    for n in range(4):
                        pa, pr = banks.next()
                        px_mm(b, n, pa, pr)
                        A("dve", lambda e, pa=pa, n=n: e.tensor_tensor(
                            out=v[:, n * 512:(n + 1) * 512], in0=pa, in1=v[:, n * 512:(n + 1) * 512], op=ALU.mult),
                          reads=[pr, "v"], writes=[("v2", n)])
                    b = take1()
                    for n in range(4):
                        pa, pr = banks.next()
                        px_mm(b, n, pa, pr)
                        s = si % 2
                        si += 1
                        A("act", lambda e, pa=pa, s=s: e.activation(out=sgm[s][:], in_=pa, func=AF.Silu),
                          reads=[pr], writes=[("sgm", s)])
                        A("dve", lambda e, s=s, j=j, n=n: e.tensor_tensor(
                            out=ycv[:, j, n * 512:(n + 1) * 512], in0=sgm[s][:], in1=v[:, n * 512:(n + 1) * 512],
                            op=ALU.mult), reads=[("sgm", s), ("v2", n)], writes=[("ycv", j, n), "v"])
                rel_w(1)
                dump("attg", attT[:], [128, 4, NTOK], BF16, [("attT", c, qc) for c in range(4) for qc in range(4)])
                dump("ycv", ycv[:], [128, 4, NTOK], BF16, [("ycv", j, n) for j in range(4) for n in range(4)])
                P.barrier(scratch[:, 0:1])

            with ExitStack() as sd2:
                sgc = [sb(sd2, "sgc%d" % i, [128, 512], F32) for i in range(2)]
                sgl = [sb(sd2, "sgl%d" % i, [128, 512], F32) for i in range(2)]
                ta = [sb(sd2, "ta%d" % i, [128, 512], F32) for i in range(2)]
                tb = [sb(sd2, "tb%d" % i, [128, 512], F32) for i in range(2)]
                si = 0
                for i in range(8):
                    bc_ = next_w()
                    bm_ = next_w()
                    for n in range(4):
                        s = si % 2
                        si += 1
                        pa, pr = banks.next()
                        px_mm(bc_, n, pa, pr)
                        A("act", lambda e, pa=pa, s=s: e.activation(out=sgc[s][:], in_=pa, func=AF.Sigmoid),
                          reads=[pr], writes=[("sgc", s)])
                        pa, pr = banks.next()
                        px_mm(bm_, n, pa, pr)
                        A("act", lambda e, pa=pa, s=s: e.activation(out=sgl[s][:], in_=pa, func=AF.Sigmoid),
                          reads=[pr], writes=[("sgl", s)])
                        pa, pr = banks.next()

                        def fn(e, pa=pa, i=i, n=n):
                            for j in range(4):
                                ins = e.matmul(pa, lhsT=woc_bf[:, j, i * 128:(i + 1) * 128],
                                               rhs=ycv[:, j, n * 512:(n + 1) * 512], start=(j == 0), stop=(j == 3))
                            return ins

                        A("pe", fn, reads=[("woc", j) for j in range(4)] + [("ycv", j, n) for j in range(4)],
                          writes=[pr])
                        A("dve", lambda e, pa=pa, s=s: e.tensor_tensor(out=ta[s][:], in0=pa, in1=sgc[s][:],
                                                                       op=ALU.mult),
                          reads=[pr, ("sgc", s)], writes=[("ta", s)])
                        pa, pr = banks.next()

                        def fn(e, pa=pa, i=i, n=n):
                            for j in range(4):
                                ins = e.matmul(pa, lhsT=wom_bf[:, j, i * 128:(i + 1) * 128],
                                               rhs=attT[:, j, n * 512:(n + 1) * 512], start=(j == 0), stop=(j == 3))
                            return ins

                        A("pe", fn, reads=[("wom", j) for j in range(4)] + [("attT", j, n) for j in range(4)],
                          writes=[pr])
                        A("dve", lambda e, pa=pa, s=s: e.tensor_tensor(out=tb[s][:], in0=pa, in1=sgl[s][:],
                                                                       op=ALU.mult),
                          reads=[pr, ("sgl", s)], writes=[("tb", s)])
                        A("dve", lambda e, s=s, i=i, n=n: e.tensor_tensor(
                            out=merged[:, i, n * 512:(n + 1) * 512], in0=ta[s][:], in1=tb[s][:], op=ALU.add),
                          reads=[("ta", s), ("tb", s)], writes=[("merged", i, n)])
                    rel_w(2)
                dump("merged", merged[:], [128, 8, NTOK], BF16, [("merged", i, n) for i in range(8) for n in range(4)])
                P.barrier(scratch[:, 0:1])

            with ExitStack() as sd3:
                NS = 4
                xs2 = [sb(sd3, "xs2_%d" % i, [128, D], F32) for i in range(NS)]
                rr = [sb(sd3, "rr%d" % i, [128, D], F32) for i in range(NS)]
                lng = sb(sd3, "lng", [128, D], F32)
                lnb = sb(sd3, "lnb", [128, D], F32)
                st6 = [sb(sd3, "st6_%d" % i, [128, 12], F32) for i in range(NS)]
                mv = [sb(sd3, "mv%d" % i, [128, 8], F32) for i in range(NS)]
                junk = stg[0]
                A("sp", dma(lng[:], lng_d.partition_broadcast(128)), writes=["lng"], dma=True)
                A("sp", dma(lnb[:], lnb_d.partition_broadcast(128)), writes=["lnb"], dma=True)
                out_ops = []

                def load_x(tt):
                    A("sp", dma(xs2[tt % NS][:], x_d[tt * 128:(tt + 1) * 128, :]), writes=[("xs2", tt % NS)], dma=True)

                for tt in range(NS):
                    load_x(tt)

                def early(tt):
                    s = tt % NS
                    pa = pb_t[tt % 4]
                    pr = ("bank", 2 * (tt % 4))
                    pr2 = ("bank", 2 * (tt % 4) + 1)

                    def fn(e):
                        for hf in range(2):
                            for kc in range(8):
                                ins = e.matmul(pa[:, hf * 512:(hf + 1) * 512], lhsT=merged[:, kc, tt * 128:(tt + 1) * 128],
                                               rhs=wo_bf[:, kc, hf * 512:(hf + 1) * 512], start=(kc == 0), stop=(kc == 7))
                        return ins

                    A("pe", fn, reads=[("merged", i, tt // 4) for i in range(8)] + [("wo", k) for k in range(8)],
                      writes=[pr, pr2])
                    A("dve", lambda e: e.scalar_tensor_tensor(out=rr[s][:], in0=xs2[s][:], scalar=ALPHA,
                                                              in1=pa[:], op0=ALU.mult, op1=ALU.add,
                                                              accum_out=mv[s][:, 0:1]),
                      reads=[pr, pr2, ("xs2", s)], writes=[("rr", s), ("mv", s, 0)])
                    if tt + NS < 16:
                        load_x(tt + NS)
                    A("act", lambda e: e.activation(out=junk[:], in_=rr[s][:], func=AF.Square,
                                                    accum_out=mv[s][:, 1:2]),
                      reads=[("rr", s)], writes=["junk", ("mv", s, 1)])
                    A("dve", lambda e: e.tensor_scalar_mul(out=mv[s][:, 4:5], in0=mv[s][:, 0:1], scalar1=1.0 / D),
                      reads=[("mv", s, 0)], writes=[("mv", s, 4)])
                    A("dve", lambda e: e.scalar_tensor_tensor(out=mv[s][:, 5:6], in0=mv[s][:, 4:5], scalar=-1.0,
                                                              in1=mv[s][:, 4:5], op0=ALU.mult, op1=ALU.mult),
                      reads=[("mv", s, 4)], writes=[("mv", s, 5)])
                    A("dve", lambda e: e.scalar_tensor_tensor(out=mv[s][:, 6:7], in0=mv[s][:, 1:2], scalar=1.0 / D,
                                                              in1=mv[s][:, 5:6], op0=ALU.mult, op1=ALU.add),
                      reads=[("mv", s, 1), ("mv", s, 5)], writes=[("mv", s, 6)])
                    A("act", lambda e: e.activation(out=mv[s][:, 2:3], in_=mv[s][:, 6:7], func=AF.Sqrt,
                                                    bias=consts[:, 1:2], scale=1.0),
                      reads=[("mv", s, 6), "consts"], writes=[("mv2", s)])
                    A("dve", lambda e: e.reciprocal(out=mv[s][:, 2:3], in_=mv[s][:, 2:3]),
                      reads=[("mv2", s)], writes=[("mv2", s)])
                    A("dve", lambda e: e.scalar_tensor_tensor(out=mv[s][:, 3:4], in0=mv[s][:, 4:5], scalar=-1.0,
                                                              in1=mv[s][:, 2:3], op0=ALU.mult, op1=ALU.mult),
                      reads=[("mv", s, 4), ("mv2", s)], writes=[("mv3", s)])
                    A("act", lambda e: e.activation(out=rr[s][:], in_=rr[s][:], func=AF.Identity,
                                                    bias=mv[s][:, 3:4], scale=mv[s][:, 2:3]),
                      reads=[("rr", s), ("mv2", s), ("mv3", s)], writes=[("rr", s)])

                def late(tt):
                    s = tt % NS
                    A("pool", lambda e: e.tensor_tensor(out=rr[s][:], in0=rr[s][:], in1=lng[:], op=ALU.mult),
                      reads=[("rr", s), "lng"], writes=[("rr", s)])
                    A("dve",
                      lambda e: e.tensor_tensor(out=rr[s][:], in0=rr[s][:], in1=lnb[:], op=ALU.add),
                      reads=[("rr", s), "lnb"], writes=[("rr", s)])
                    out_ops.append(A("sp", dma(out_d[tt * 128:(tt + 1) * 128, :], rr[s][:]), reads=[("rr", s)],
                                     dma=True))

                SKEW = 2
                for tt in range(16 + SKEW):
                    if tt < 16:
                        early(tt)
                    if tt - SKEW >= 0:
                        late(tt - SKEW)
                A("sp", None, deps=out_ops + list(dbg_outs.values()))
    return nc


_CACHE = {}


def _rope_tables():
    grid_w = 64
    t = np.arange(NTOK)
    row = (t // grid_w).astype(np.float32)
    col = (t % grid_w).astype(np.float32)
    inv = (np.float32(10000.0) ** (-np.arange(0, 16, 2, dtype=np.float32) / np.float32(16))).astype(np.float32)
    ar = row[:, None] * inv[None, :]
    ac = col[:, None] * inv[None, :]
    cr, sr, cc, sc = np.cos(ar), np.sin(ar), np.cos(ac), np.sin(ac)
    C = np.concatenate([cr, cr, cc, cc], axis=1).astype(np.float32)
    S = np.concatenate([-sr, sr, -sc, sc], axis=1).astype(np.float32)
    return C, S


def _swap32(a):
    return np.concatenate([a[..., 8:16], a[..., 0:8], a[..., 24:32], a[..., 16:24]], axis=-1)


def kernel(x, c, ctx, c_ctx, w_ada, b_ada, w_in, conv_w, q_norm_g, w_uq, kv_norm_g, w_ukv,
           w_out_conv, w_out_mla, w_o, ln_g, ln_b, _debug=()):
    f = np.float32
    x = np.asarray(x, f); c = np.asarray(c, f); ctx = np.asarray(ctx, f); c_ctx = np.asarray(c_ctx, f)
    w_ada = np.ascontiguousarray(np.asarray(w_ada, f)[0]); b_ada = np.asarray(b_ada, f)[0]
    w_in = np.ascontiguousarray(np.asarray(w_in, f)[0]); conv_w = np.asarray(conv_w, f)[0]
    q_norm_g = np.asarray(q_norm_g, f)[0]; w_uq = np.asarray(w_uq, f)[0]
    kv_norm_g = np.asarray(kv_norm_g, f)[0]; w_ukv = np.asarray(w_ukv, f)[0]
    w_out_conv = np.ascontiguousarray(np.asarray(w_out_conv, f)[0])
    w_out_mla = np.ascontiguousarray(np.asarray(w_out_mla, f)[0])
    w_o = np.ascontiguousarray(np.asarray(w_o, f)[0]); ln_g = np.asarray(ln_g, f)[0]; ln_b = np.asarray(ln_b, f)[0]
    B = x.shape[0]

    C, S = _rope_tables()
    tabq = np.ascontiguousarray(np.concatenate([np.ones((64, NTOK), f), C.T, S.T], axis=0))
    Ck = np.concatenate([np.ones((32, NCTX), f), C.T], axis=1)
    Sk = np.concatenate([np.zeros((32, NCTX), f), S.T], axis=1)
    tabk = np.ascontiguousarray(np.stack([np.concatenate([Ck, Ck], 0), np.concatenate([Sk, Sk], 0)], 0))
    kr = w_in[:, 2432:2464]
    krs = _swap32(kr)
    w_kr = np.ascontiguousarray(np.concatenate([kr] * 4 + [krs] * 4, axis=1))
    wq = w_uq.reshape(256, 8, 96)
    w_uq_aug = np.ascontiguousarray(
        np.concatenate([wq[:, :, 0:64], wq[:, :, 64:96], _swap32(wq[:, :, 64:96])], axis=2).reshape(256, 1024))
    wkv = w_ukv.reshape(128, 8, 128)
    w_ukv_r = np.ascontiguousarray(
        np.concatenate([wkv[:, :, 0:64].reshape(128, 512), wkv[:, :, 64:128].reshape(128, 512)], axis=1))
    bada = b_ada[0:2048].reshape(16, 128).T
    bada2 = np.ascontiguousarray(np.repeat(bada[:, :, None], 2, axis=2).reshape(128, 32))
    badag = np.ascontiguousarray(b_ada[2048:3072])
    convw = np.ascontiguousarray(conv_w.reshape(3, 4, 128).transpose(2, 1, 0).reshape(128, 12))
    gq = np.ascontiguousarray(q_norm_g.reshape(2, 128).T)
    gkv = np.ascontiguousarray(kv_norm_g.reshape(1, 128).T)
    cctx_cols = c_ctx.reshape(8, 128).T

    key = tuple(sorted(_debug))
    if key not in _CACHE:
        _CACHE[key] = build_program(debug=_debug)
    nc = _CACHE[key]
    in_maps = []
    for b in range(B):
        cc = np.stack([c[b].reshape(8, 128).T, cctx_cols], axis=2).reshape(128, 16)
        in_maps.append({
            "x": np.ascontiguousarray(x[b]), "ctx": np.ascontiguousarray(ctx[b]), "cc": np.ascontiguousarray(cc),
            "w_ada": w_ada, "bada": bada2, "badag": badag, "w_in": w_in, "w_kr": w_kr, "convw": convw,
            "gq": gq, "gkv": gkv, "w_uq": w_uq_aug, "w_ukv": w_ukv_r, "w_oc": w_out_conv, "w_om": w_out_mla,
            "w_o": w_o, "ln_g": ln_g, "ln_b": ln_b, "tabq": tabq, "tabk": tabk,
        })
    res = run_bass_kernel_spmd(nc, in_maps, core_ids=list(range(B)))
    out = np.stack([np.asarray(r["out"], f) for r in res.results], axis=0)
    if _debug:
        return out, res.results
    return out
```

```python
import numpy as np
import concourse.bass as bass
import concourse.mybir as mybir
from concourse.bass_utils import run_bass_kernel_spmd

F32 = mybir.dt.float32
BF16 = mybir.dt.bfloat16
AF = mybir.ActivationFunctionType
ALU = mybir.AluOpType

D = 1024
NTOK = 2048
NCTX = 256
NKEY = NTOK + NCTX
NIN = 5024
LN_EPS = 1e-5
RMS_EPS = 1e-6
ALPHA = 2.0 ** 0.25
SM_SCALE = 96.0 ** -0.5
ND = 32


class Op:
    __slots__ = ("eng", "dma", "sem", "val", "idx", "key")


class Prog:
    ENGS = ("sp", "act", "pool", "pe", "dve")

    def __init__(self, nc, sems, dma_sems):
        self.nc = nc
        self.eobj = {"sp": nc.sync, "act": nc.scalar, "pool": nc.gpsimd, "pe": nc.tensor, "dve": nc.vector}
        self.sems = sems
        self.dma_sems = dma_sems
        self.cnt = {e: 0 for e in self.ENGS}
        self.nops = {e: 0 for e in self.ENGS}
        self.known = {e: {} for e in self.ENGS}
        self.lastw = {}
        self.readers = {}
        self.barrier_op = None
        self.dmas = []
        self.pending_dmas = []
        self.last_op = {}

    def _needs_wait(self, eng, is_dma, idx, d):
        if d.dma:
            return True
        if d.eng != eng:
            return True
        if is_dma:
            return True
        if eng == "pe":
            return False
        return (idx - d.idx) <= 2

    def add(self, eng, fn, reads=(), writes=(), dma=False, deps=()):
        deps = set(deps)
        if self.barrier_op is not None and eng != "pe":
            deps.add(self.barrier_op)
        for r in reads:
            deps |= self.lastw.get(r, set())
        for w in writes:
            deps |= self.lastw.get(w, set())
            deps |= self.readers.get(w, set())
        op = Op()
        op.eng = eng
        op.dma = dma
        op.idx = self.nops[eng]
        self.nops[eng] += 1
        if dma:
            i = len(self.dmas)
            if i >= ND:
                deps.add(self.dmas[i - ND])
            op.sem = self.dma_sems[i % ND]
            op.val = 16 * (i // ND + 1)
            op.key = ("d", i % ND)
            self.dmas.append(op)
            self.pending_dmas.append(op)
        else:
            op.sem = self.sems[eng]
            op.key = ("e", eng)
            op.val = None
        e = self.eobj[eng]
        need = {}
        for d in deps:
            if not self._needs_wait(eng, dma, op.idx, d):
                continue
            if need.get(d.key, (None, 0))[1] < d.val:
                need[d.key] = (d.sem, d.val)
        kn = self.known[eng]
        for k, (sem, val) in need.items():
            if kn.get(k, 0) < val:
                e.wait_ge(sem, val)
                kn[k] = val
        if fn is not None:
            ins = fn(e)
            if dma:
                ins.then_inc(op.sem, 16)
            else:
                self.cnt[eng] += 1
                op.val = self.cnt[eng]
                ins.then_inc(op.sem, 1)
        else:
            op.val = self.cnt[eng]
        for r in reads:
            self.readers.setdefault(r, set()).add(op)
        for w in writes:
            if self.readers.get(w):
                self.lastw[w] = {op}
                self.readers[w] = set()
            else:
                self.lastw.setdefault(w, set()).add(op)
        self.last_op[eng] = op
        return op

    def barrier(self, scratch_ap):
        deps = set(self.last_op.values()) | set(self.pending_dmas)
        b = self.add("dve", lambda e: e.memset(scratch_ap, 0.0), deps=deps)
        self.barrier_op = b
        self.pending_dmas = []
        return b


class Ring:
    def __init__(self, items):
        self.items = items
        self.i = 0

    def next(self):
        it = self.items[self.i % len(self.items)]
        self.i += 1
        return it


def build_program(debug=()):
    nc = bass.Bass("TRN2", target_bir_lowering=False)

    def din(name, shape):
        return nc.dram_tensor(name, list(shape), F32, kind="ExternalInput").ap()

    x_d = din("x", [NTOK, D])
    ctx_d = din("ctx", [NCTX, D])
    cc_d = din("cc", [128, 16])
    wada_d = din("w_ada", [D, 3 * D])
    bada_d = din("bada", [128, 32])
    badag_d = din("badag", [D])
    win_d = din("w_in", [D, NIN])
    wkr_d = din("w_kr", [D, 256])
    convw_d = din("convw", [128, 12])
    gq_d = din("gq", [128, 2])
    gkv_d = din("gkv", [128, 1])
    wuq_d = din("w_uq", [256, 1024])
    wukv_d = din("w_ukv", [128, 1024])
    woc_d = din("w_oc", [512, D])
    wom_d = din("w_om", [512, D])
    wo_d = din("w_o", [D, D])
    lng_d = din("ln_g", [D])
    lnb_d = din("ln_b", [D])
    tabq_d = din("tabq", [128, NTOK])
    tabk_d = din("tabk", [2, 64, NKEY])
    out_d = nc.dram_tensor("out", [NTOK, D], F32, kind="ExternalOutput").ap()

    dbg_outs = {}

    from contextlib import ExitStack

    with ExitStack() as top:
        sems = {e: top.enter_context(nc.semaphore("s_" + e)) for e in Prog.ENGS}
        dma_sems = [top.enter_context(nc.semaphore("d%d" % i)) for i in range(ND)]
        P = Prog(nc, sems, dma_sems)
        A = P.add

        def sb(stack, name, shape, dt):
            return stack.enter_context(nc.sbuf_tensor(name, list(shape), dt))

        def ps(stack, name, shape, dt=F32):
            return stack.enter_context(nc.psum_tensor(name, list(shape), dt))

        def dma(out, in_):
            return lambda e: e.dma_start(out=out, in_=in_)

        def dump(name, ap, shape, dt, reads):
            if name not in debug:
                return
            t = nc.dram_tensor("dbg_" + name, list(shape), dt, kind="ExternalOutput").ap()
            dbg_outs[name] = A("sp", dma(t, ap), reads=reads, dma=True)

        hxT = sb(top, "hxT", [128, 8, NTOK], BF16)
        hcT = sb(top, "hcT", [128, 8, NCTX], BF16)
        attT = sb(top, "attT", [128, 4, NTOK], BF16)
        ident = sb(top, "ident", [128, 128], BF16)
        ones_bf = sb(top, "ones_bf", [128, 128], BF16)
        cc_sb = sb(top, "cc_sb", [128, 16], F32)
        cs = sb(top, "cs", [128, 16], F32)
        mod = sb(top, "mod", [128, 32], F32)
        bada_sb = sb(top, "bada_sb", [128, 32], F32)
        consts = sb(top, "consts", [128, 8], F32)
        scratch = sb(top, "scratch", [128, 8], F32)
        convw_sb = sb(top, "convw_sb", [128, 12], F32)
        gq_sb = sb(top, "gq_sb", [128, 2], F32)
        gkv_sb = sb(top, "gkv_sb", [128, 1], F32)
        wst = [sb(top, "wst%d" % i, [128, 8, 128], F32) for i in range(3)]
        wbf = [sb(top, "wbf%d" % i, [128, 8, 128], BF16) for i in range(5)]
        wst_r = Ring([0, 1, 2])
        wbf_r = Ring([0, 1, 2, 3, 4])
        gate_bc = sb(top, "gate_bc", [128, D], F32)
        pb_t = [ps(top, "pb%d" % i, [128, 1024]) for i in range(4)]

        def bank_f32(i):
            return pb_t[i // 2][:, (i % 2) * 512:(i % 2) * 512 + 512]

        def bank_bf16(i):
            return bank_f32(i).bitcast(BF16)

        all_banks = [(bank_f32(i), ("bank", i)) for i in range(8)]

        A("pool", lambda e: e.memset(ident[:], 0.0), writes=["ident"])
        A("pool", lambda e: e.affine_select(out=ident[:], in_=ident[:], pattern=[[-1, 128]],
                                            compare_op=ALU.not_equal, fill=1.0, base=0, channel_multiplier=1),
          reads=["ident"], writes=["ident"])
        A("pool", lambda e: e.memset(ones_bf[:], 1.0), writes=["ones"])
        A("pool", lambda e: e.memset(consts[:, 0:1], RMS_EPS), writes=["consts"])
        A("pool", lambda e: e.memset(consts[:, 1:2], LN_EPS), writes=["consts"])
        A("sp", dma(cc_sb[:], cc_d), writes=["cc"], dma=True)
        A("sp", dma(bada_sb[:], bada_d), writes=["bada"], dma=True)
        A("sp", dma(convw_sb[:], convw_d), writes=["convw"], dma=True)
        A("sp", dma(gq_sb[:], gq_d), writes=["gq"], dma=True)
        A("sp", dma(gkv_sb[:], gkv_d), writes=["gkv"], dma=True)
        A("act", lambda e: e.activation(out=cs[:], in_=cc_sb[:], func=AF.Silu), reads=["cc"], writes=["cs"])

        wsrcs = [win_d[:, 2304:2432], wkr_d[:, 0:128], wkr_d[:, 128:256], win_d[:, 2048:2176], win_d[:, 2176:2304]]
        wsrcs += [win_d[:, 2464 + c * 128:2464 + (c + 1) * 128] for c in range(4)]
        for j in range(4):
            wsrcs += [win_d[:, o + j * 128:o + (j + 1) * 128] for o in (0, 1024, 512, 1536)]
        for i in range(8):
            wsrcs += [win_d[:, o + i * 128:o + (i + 1) * 128] for o in (2976, 4000)]
        ws_state = {"issued": 0, "taken": 0, "released": 0, "slots": [], "cast": "act", "hook": None}
        AHEAD = 3

        def _issue_w():
            k = ws_state["issued"]
            src = wsrcs[k]
            ws_state["issued"] += 1
            s = wst_r.next()
            b = wbf_r.next()
            A("sp", dma(wst[s][:], src.rearrange("(k p) m -> p k m", p=128)), writes=[("wst", s)], dma=True)
            ce = ws_state["cast"]
            if ce == "act":
                A("act", lambda e: e.copy(out=wbf[b][:], in_=wst[s][:]), reads=[("wst", s)], writes=[("wbf", b)])
            else:
                A(ce, lambda e: e.tensor_copy(out=wbf[b][:], in_=wst[s][:]), reads=[("wst", s)], writes=[("wbf", b)])
            ws_state["slots"].append(b)
            if ws_state["hook"] is not None:
                ws_state["hook"]()

        def ws_prefetch(n_ahead=AHEAD):
            while (ws_state["issued"] < len(wsrcs) and ws_state["issued"] - ws_state["taken"] < n_ahead
                   and ws_state["issued"] - ws_state["released"] < len(wbf)):
                _issue_w()

        def next_w():
            if ws_state["issued"] == ws_state["taken"]:
                assert ws_state["issued"] - ws_state["released"] < len(wbf)
                _issue_w()
            b = ws_state["slots"].pop(0)
            ws_state["taken"] += 1
            ws_prefetch()
            return b

        def rel_w(n=1):
            ws_state["released"] += n
            ws_prefetch()

        def px_mm(b, n, ps_ap, ps_res, is_ctx=False, m0=0, m1=128):
            if is_ctx:
                rhs = [hcT[:, kc, :] for kc in range(8)]
                reads = [("hcT", kc) for kc in range(8)]
            else:
                rhs = [hxT[:, kc, n * 512:(n + 1) * 512] for kc in range(8)]
                reads = [("hxT", kc, n // 2) for kc in range(8)]

            def fn(e):
                for kc in range(8):
                    ins = e.matmul(ps_ap, lhsT=wbf[b][:, kc, m0:m1], rhs=rhs[kc], start=(kc == 0), stop=(kc == 7))
                return ins

            return A("pe", fn, reads=reads + [("wbf", b)], writes=[ps_res])

        with ExitStack() as s0:
            NXS = 8
            wa = [sb(s0, "wa%d" % i, [128, 2048], F32) for i in range(2)]
            wab = [sb(s0, "wab%d" % i, [128, 2048], BF16) for i in range(2)]
            xs = [sb(s0, "xs%d" % i, [128, D], F32) for i in range(NXS)]
            xbf = [sb(s0, "xbf%d" % i, [128, D], BF16) for i in range(NXS)]
            cs_bf = sb(s0, "cs_bf", [128, 16], BF16)
            modps = bank_f32(7)

            A("dve", lambda e: e.tensor_copy(out=cs_bf[:], in_=cs[:]), reads=["cs"], writes=["cs_bf"])

            EVQ = {0: (0, 0), 2: (0, 1), 4: (0, 2), 1: (1, 0), 3: (1, 1), 5: (1, 2), 6: (1, 3), 7: (1, 4)}
            tiles = [(ctx_d[t * 128:(t + 1) * 128, :], hcT, t * 128, 1) for t in range(2)]
            tiles += [(x_d[j * 128:(j + 1) * 128, :], hxT, j * 128, 0) for j in range(16)]
            NT = len(tiles)

            def load_tile(j):
                s = j % NXS
                A("sp", dma(xs[s][:], tiles[j][0]), writes=[("xs", s)], dma=True)

            def cast_tile(j):
                s = j % NXS
                if j % 2 == 0:
                    A("act", lambda e: e.copy(out=xbf[s][:], in_=xs[s][:]), reads=[("xs", s)], writes=[("xbf", s)])
                else:
                    A("dve", lambda e: e.tensor_copy(out=xbf[s][:], in_=xs[s][:]),
                      reads=[("xs", s)], writes=[("xbf", s)])

            groups = [[0, 1]] + [list(range(2 + 4 * g, 6 + 4 * g)) for g in range(4)]
            tpb = Ring([(bank_bf16(i), ("bank", i)) for i in range(6)])

            def do_group(g):
                tl = groups[g]
                _, dstT, c0, mj = tiles[tl[0]]
                W = 128 * len(tl)
                for kc in range(8):
                    bank, bres = tpb.next()

                    def fn(e, bank=bank, kc=kc):
                        for i, j in enumerate(tl):
                            ins = e.transpose(bank[:, i * 128:(i + 1) * 128],
                                              xbf[j % NXS][:, kc * 128:(kc + 1) * 128], ident[:])
                        return ins

                    A("pe", fn, reads=[("xbf", j % NXS) for j in tl] + ["ident"], writes=[bres])
                    dst = dstT[:, kc, c0:c0 + W]
                    sc_ap = mod[:, 16 + 2 * kc + mj:16 + 2 * kc + mj + 1]
                    sh_ap = mod[:, 2 * kc + mj:2 * kc + mj + 1]
                    res = ("hxT", kc, c0 // 1024) if mj == 0 else ("hcT", kc)
                    if kc % 2 == 0:
                        A("act", lambda e, dst=dst, bank=bank, sc_ap=sc_ap, sh_ap=sh_ap: e.activation(
                            out=dst, in_=bank[:, 0:W], func=AF.Identity, bias=sh_ap, scale=sc_ap),
                          reads=[bres, "mod"], writes=[res])
                    else:
                        A("dve", lambda e, dst=dst, bank=bank, sc_ap=sc_ap, sh_ap=sh_ap: e.tensor_scalar(
                            out=dst, in0=bank[:, 0:W], scalar1=sc_ap, scalar2=sh_ap, op0=ALU.mult, op1=ALU.add),
                          reads=[bres, "mod"], writes=[res])

            for kc in range(8):
                s = kc % 2
                A("sp", dma(wa[s][:], wada_d[kc * 128:(kc + 1) * 128, 0:2048]), writes=[("wa", s)], dma=True)
                A("dve", lambda e, s=s: e.tensor_copy(out=wab[s][:], in_=wa[s][:]), reads=[("wa", s)],
                  writes=[("wab", s)])

                def fn(e, kc=kc, s=s):
                    for m in range(16):
                        ins = e.matmul(modps[:, 2 * m:2 * m + 2], lhsT=wab[s][:, m * 128:(m + 1) * 128],
                                       rhs=cs_bf[:, 2 * kc:2 * kc + 2], start=(kc == 0 and m == 0), stop=(kc == 7),
                                       skip_group_check=True)
                    return ins

                A("pe", fn, reads=[("wab", s), "cs_bf"], writes=[("bank", 7)])
            for j in range(NXS):
                load_tile(j)
                if j == 3:
                    ws_prefetch()
            A("dve", lambda e: e.tensor_tensor(out=mod[:], in0=modps[:, 0:32], in1=bada_sb[:], op=ALU.add),
              reads=[("bank", 7), "bada"], writes=["mod"])
            A("dve", lambda e: e.tensor_scalar_add(out=mod[:, 16:32], in0=mod[:, 16:32], scalar1=1.0),
              reads=["mod"], writes=["mod"])
            for j in range(NXS):
                cast_tile(j)
                load_tile(j + NXS)
            for g in range(len(groups)):
                do_group(g)
                for t in groups[g]:
                    if t + NXS < NT:
                        cast_tile(t + NXS)
                    if t + 2 * NXS < NT:
                        load_tile(t + 2 * NXS)
            if "stopA" in debug:
                dump("hxT", hxT[:], [128, 8, NTOK], BF16, [("hxT", kc, h) for kc in range(8) for h in range(2)])
                A("sp", None, deps=list(dbg_outs.values()))
                return nc
            dump("hxT", hxT[:], [128, 8, NTOK], BF16, [("hxT", kc, h) for kc in range(8) for h in range(2)])
            dump("hcT", hcT[:], [128, 8, NCTX], BF16, [("hcT", kc) for kc in range(8)])
            dump("mod", mod[:], [128, 32], F32, ["mod"])
            P.barrier(scratch[:, 0:1])

        with ExitStack() as s1:
            kT = [sb(s1, "kT%d" % h, [128, NKEY], BF16) for h in range(8)]
            Vaug = sb(s1, "Vaug", [128, 18, 768], BF16)
            kchunks = [(0, 256, True, 0)] + [(256 + 512 * n, 512, False, n) for n in range(4)]

            with ExitStack() as skv:
                ckvnT = sb(skv, "ckvnT", [128, NKEY], BF16)
                krot = sb(skv, "krot", [128, NKEY], BF16)
                Ck = sb(skv, "Ck", [128, NKEY], F32)
                Sk = sb(skv, "Sk", [128, NKEY], F32)
                t1 = [sb(skv, "t1_%d" % i, [128, 512], F32) for i in range(2)]
                t2 = [sb(skv, "t2_%d" % i, [128, 512], F32) for i in range(2)]
                sq = [sb(skv, "sq%d" % i, [128, 512], BF16) for i in range(2)]
                rs = [sb(skv, "rs%d" % i, [128, 512], F32) for i in range(2)]
                wukv_st = sb(skv, "wukv_st", [128, 1024], F32)
                wukv_bf = sb(skv, "wukv_bf", [128, 1024], BF16)
                banks = Ring(list(all_banks))

                ws_state["cast"] = "pool"
                b = next_w()
                bA = next_w()
                bB = next_w()
                A("sp", dma(Ck[64:128, :], tabk_d[0]), writes=["Ck"], dma=True)
                A("sp", dma(Sk[64:128, :], tabk_d[1]), writes=["Sk"], dma=True)
                A("sp", dma(wukv_st[:], wukv_d), writes=["wukv_st"], dma=True)

                csrep = sb(skv, "csrep", [128, 8, 128], F32)
                wag = [sb(skv, "wag%d" % i, [128, D], F32) for i in range(2)]
                bgate = sb(skv, "bgate", [128, D], F32)
                A("sp", dma(bgate[:], badag_d.partition_broadcast(128)), writes=["bgate"], dma=True)
                A("pool", lambda e: e.memset(csrep[:], 1.0), writes=[("csrep", kc) for kc in range(8)])
                def gate_prep_dve():
                    for kc in range(8):
                        A("dve", lambda e, kc=kc: e.tensor_scalar_mul(out=csrep[:, kc, :], in0=csrep[:, kc, :],
                                                                      scalar1=cs[:, 2 * kc:2 * kc + 1]),
                          reads=["cs", ("csrep", kc)], writes=[("csrep", kc)])

                allb = banks.items
                ring6 = Ring(allb[0:6])
                gpa = allb[6:8]

                def gate_dma(kc):
                    s = kc % 2
                    A("sp", dma(wag[s][:], wada_d[kc * 128:(kc + 1) * 128, 2048:3072]), writes=[("wag", s)], dma=True)

                def gate_step(kc):
                    s = kc % 2

                    def fn(e):
                        for hf in range(2):
                            ins = e.matmul(gpa[hf][0], lhsT=csrep[:, kc, :], rhs=wag[s][:, hf * 512:(hf + 1) * 512],
                                           start=(kc == 0), stop=(kc == 7))
                        return ins

                    A("pe", fn, reads=[("csrep", kc), ("wag", s)], writes=[gpa[0][1], gpa[1][1]])
                    if kc + 2 < 8:
                        gate_dma(kc + 2)

                gate_dma(0)
                gate_dma(1)
                gate_sched = {0: [], 1: [0, 1], 2: [2, 3], 3: [4, 5], 4: [6, 7]}

                for g in range(6):
                    A("pool", lambda e, g=g: e.memset(
                        Vaug[:, 3 * g:3 * g + 3, :].rearrange("p k (c s d) -> p k c s d", c=4, s=3)[:, :, :, 1, :], 1.0),
                      writes=[("V", kt) for kt in range(3 * g, 3 * g + 3)])
                for ci, (k0, W, isc, n) in enumerate(kchunks):
                    s = ci % 2
                    pa, pr = ring6.next()
                    px_mm(b, n, pa[:, :W], pr, is_ctx=isc)
                    A("act", lambda e, pa=pa, s=s, W=W: e.activation(out=sq[s][:, :W], in_=pa[:, :W], func=AF.Square),
                      reads=[pr], writes=[("sq", s)])
                    paA, prA = ring6.next()
                    px_mm(bA, n, paA[:, :W], prA, is_ctx=isc)
                    paB, prB = ring6.next()
                    px_mm(bB, n, paB[:, :W], prB, is_ctx=isc)
                    pa2, pr2 = ring6.next()
                    A("pe", lambda e, pa2=pa2, s=s, W=W: e.matmul(pa2[:, :W], lhsT=ones_bf[:], rhs=sq[s][:, :W],
                                                                  start=True, stop=True),
                      reads=[("sq", s), "ones"], writes=[pr2])
                    A("act", lambda e, pa2=pa2, s=s, W=W: e.activation(out=rs[s][:, :W], in_=pa2[:, :W], func=AF.Ln,
                                                                       bias=consts[:, 0:1], scale=1.0 / 128.0),
                      reads=[pr2, "consts"], writes=[("rs", s)])
                    A("act", lambda e, s=s, W=W: e.activation(out=rs[s][:, :W], in_=rs[s][:, :W], func=AF.Exp,
                                                              scale=-0.5),
                      reads=[("rs", s)], writes=[("rs", s)])
                    A("dve", lambda e, paA=paA, s=s, W=W, k0=k0: e.tensor_tensor(
                        out=t1[s][64:128, :W], in0=paA[64:128, :W], in1=Ck[64:128, k0:k0 + W], op=ALU.mult),
                      reads=[prA, "Ck"], writes=[("t1", s)])
                    A("dve", lambda e, paB=paB, s=s, W=W, k0=k0: e.tensor_tensor(
                        out=t2[s][64:128, :W], in0=paB[64:128, :W], in1=Sk[64:128, k0:k0 + W], op=ALU.mult),
                      reads=[prB, "Sk"], writes=[("t2", s)])
                    A("dve", lambda e, s=s, W=W, k0=k0: e.tensor_tensor(
                        out=krot[64:128, k0:k0 + W], in0=t1[s][64:128, :W], in1=t2[s][64:128, :W], op=ALU.add),
                      reads=[("t1", s), ("t2", s)], writes=[("krot", ci)])
                    A("dve", lambda e, pa=pa, s=s, W=W, k0=k0: e.tensor_tensor(
                        out=ckvnT[:, k0:k0 + W], in0=pa[:, :W], in1=rs[s][:, :W], op=ALU.mult),
                      reads=[pr, ("rs", s)], writes=[("ckvn", ci)])
                    if ci == 0:
                        gate_prep_dve()
                    for kc in gate_sched[ci]:
                        gate_step(kc)
                rel_w(3)
                A("dve", lambda e: e.tensor_scalar_mul(out=wukv_bf[:], in0=wukv_st[:], scalar1=gkv_sb[:, 0:1]),
                  reads=["wukv_st", "gkv"], writes=["wukv_bf"])
                for h in range(8):
                    A("sp", dma(kT[h][64:128, :], krot[64:128, :]), reads=[("krot", ci) for ci in range(5)],
                      writes=[("kT", h, ci, 1) for ci in range(5)], dma=True)
                for hf in range(2):
                    A("dve", lambda e, hf=hf: e.tensor_tensor(out=gate_bc[:, hf * 512:(hf + 1) * 512], in0=gpa[hf][0],
                                                              in1=bgate[:, hf * 512:(hf + 1) * 512], op=ALU.add),
                      reads=[gpa[hf][1], "bgate"], writes=[("gate", hf)])
                dump("gate", gate_bc[:], [128, D], F32, [("gate", 0), ("gate", 1)])
                for h in range(8):
                    for ci, (k0, W, isc, n) in enumerate(kchunks):
                        pa, pr = ring6.next()
                        A("pe", lambda e, pa=pa, h=h, W=W, k0=k0: e.matmul(
                            pa[0:64, :W], lhsT=wukv_bf[:, h * 64:(h + 1) * 64], rhs=ckvnT[:, k0:k0 + W],
                            start=True, stop=True), reads=["wukv_bf", ("ckvn", ci)], writes=[pr])
                        if (h + ci) % 2 == 0:
                            A("act", lambda e, pa=pa, h=h, W=W, k0=k0: e.copy(out=kT[h][0:64, k0:k0 + W],
                                                                              in_=pa[0:64, :W]),
                              reads=[pr], writes=[("kT", h, ci, 0)])
                        else:
                            A("dve", lambda e, pa=pa, h=h, W=W, k0=k0: e.tensor_copy(out=kT[h][0:64, k0:k0 + W],
                                                                                     in_=pa[0:64, :W]),
                              reads=[pr], writes=[("kT", h, ci, 0)])
                for kt in range(18):
                    ci = 0 if kt < 2 else 1 + (kt - 2) // 4
                    pa, pr = ring6.next()
                    A("pe", lambda e, pa=pa, kt=kt: e.matmul(pa[:, 0:512], lhsT=ckvnT[:, kt * 128:(kt + 1) * 128],
                                                             rhs=wukv_bf[:, 512:1024], start=True, stop=True),
                      reads=["wukv_bf", ("ckvn", ci)], writes=[pr])
                    dst = Vaug[:, kt, :].rearrange("p (c s d) -> p c s d", c=4, s=3)[:, :, 0:3:2, :]
                    src = pa[:, 0:512].rearrange("p (c t d) -> p c t d", c=4, t=2)
                    if kt % 2 == 0:
                        A("act", lambda e, dst=dst, src=src: e.copy(out=dst, in_=src), reads=[pr], writes=[("V", kt)])
                    else:
                        A("dve", lambda e, dst=dst, src=src: e.tensor_copy(out=dst, in_=src), reads=[pr],
                          writes=[("V", kt)])

                dump("ckvnT", ckvnT[:], [128, NKEY], BF16, [("ckvn", ci) for ci in range(5)])
                dump("kT0", kT[0][:], [128, NKEY], BF16, [("kT", 0, ci, j) for ci in range(5) for j in range(2)])
                dump("kT3", kT[3][:], [128, NKEY], BF16, [("kT", 3, ci, j) for ci in range(5) for j in range(2)])
                dump("Vaug", Vaug[:], [128, 18, 768], BF16, [("V", kt) for kt in range(18)])
                if "stopKV" in debug:
                    A("sp", None, deps=list(dbg_outs.values()) + list(P.last_op.values()) + P.pending_dmas)
                    return nc
                P.barrier(scratch[:, 0:1])

            qT = [sb(s1, "qT%d" % h, [128, NTOK], BF16) for h in range(8)]
            with ExitStack() as sq_:
                cqnT = sb(sq_, "cqnT", [128, 2, NTOK], BF16)
                Tq = sb(sq_, "Tq", [128, NTOK], F32)
                wuq_st = sb(sq_, "wuq_st", [128, 1024], F32)
                wuq_bf = sb(sq_, "wuq_bf", [128, 2, 1024], BF16)
                sq = [sb(sq_, "sqq%d" % i, [128, 512], BF16) for i in range(2)]
                rs = [sb(sq_, "rsq%d" % i, [128, 512], F32) for i in range(2)]
                banks = Ring(list(all_banks))

                A("sp", dma(Tq[:], tabq_d), writes=["Tq"], dma=True)
                for k in range(2):
                    A("sp", dma(wuq_st[:], wuq_d[k * 128:(k + 1) * 128, :]), writes=["wuq_st"], dma=True)
                    A("dve", lambda e, k=k: e.tensor_scalar_mul(out=wuq_bf[:, k, :], in0=wuq_st[:],
                                                                scalar1=gq_sb[:, k:k + 1]),
                      reads=["wuq_st", "gq"], writes=[("wuq_bf", k)])
                b0 = next_w()
                b1 = next_w()
                def q_finish(n, pa0, pr0, pa1, pr1):
                    s0_, s1_ = 0, 1
                    pa2, pr2 = banks.next()

                    def fn(e):
                        e.matmul(pa2, lhsT=ones_bf[:], rhs=sq[s0_][:], start=True, stop=False)
                        return e.matmul(pa2, lhsT=ones_bf[:], rhs=sq[s1_][:], start=False, stop=True)

                    A("pe", fn, reads=[("sq", s0_), ("sq", s1_), "ones"], writes=[pr2])
                    r = n % 2
                    A("act", lambda e: e.activation(out=rs[r][:], in_=pa2, func=AF.Ln,
                                                    bias=consts[:, 0:1], scale=1.0 / 256.0),
                      reads=[pr2, "consts"], writes=[("rs", r)])
                    A("act", lambda e: e.activation(out=rs[r][:], in_=rs[r][:], func=AF.Exp, scale=-0.5),
                      reads=[("rs", r)], writes=[("rs", r)])
                    A("dve", lambda e: e.tensor_tensor(
                        out=cqnT[:, 0, n * 512:(n + 1) * 512], in0=pa0, in1=rs[r][:], op=ALU.mult),
                      reads=[pr0, ("rs", r)], writes=[("cqn", 0, n)])
                    A("dve", lambda e: e.tensor_tensor(
                        out=cqnT[:, 1, n * 512:(n + 1) * 512], in0=pa1, in1=rs[r][:], op=ALU.mult),
                      reads=[pr1, ("rs", r)], writes=[("cqn", 1, n)])

                pend = None
                for n in range(4):
                    pa0, pr0 = banks.next()
                    px_mm(b0, n, pa0, pr0)
                    pa1, pr1 = banks.next()
                    px_mm(b1, n, pa1, pr1)
                    if pend is not None:
                        q_finish(*pend)
                    A("act", lambda e, pa0=pa0: e.activation(out=sq[0][:], in_=pa0, func=AF.Square),
                      reads=[pr0], writes=[("sq", 0)])
                    A("act", lambda e, pa1=pa1: e.activation(out=sq[1][:], in_=pa1, func=AF.Square),
                      reads=[pr1], writes=[("sq", 1)])
                    pend = (n, pa0, pr0, pa1, pr1)
                q_finish(*pend)
                rel_w(2)
                for h in range(8):
                    for n in range(4):
                        pa, pr = banks.next()

                        def fn(e, pa=pa, h=h, n=n):
                            e.matmul(pa, lhsT=wuq_bf[:, 0, h * 128:(h + 1) * 128], rhs=cqnT[:, 0, n * 512:(n + 1) * 512],
                                     start=True, stop=False)
                            return e.matmul(pa, lhsT=wuq_bf[:, 1, h * 128:(h + 1) * 128],
                                            rhs=cqnT[:, 1, n * 512:(n + 1) * 512], start=False, stop=True)

                        A("pe", fn, reads=[("wuq_bf", 0), ("wuq_bf", 1), ("cqn", 0, n), ("cqn", 1, n)], writes=[pr])
                        A("dve", lambda e, pa=pa, h=h, n=n: e.tensor_tensor(
                            out=qT[h][:, n * 512:(n + 1) * 512], in0=pa, in1=Tq[:, n * 512:(n + 1) * 512], op=ALU.mult),
                          reads=[pr, "Tq"], writes=[("qT", h, n)])
                dump("qT0", qT[0][:], [128, NTOK], BF16, [("qT", 0, n) for n in range(4)])
                dump("qT5", qT[5][:], [128, NTOK], BF16, [("qT", 5, n) for n in range(4)])
                if "stopQ" in debug:
                    A("sp", None, deps=list(dbg_outs.values()) + list(P.last_op.values()) + P.pending_dmas)
                    return nc
                P.barrier(scratch[:, 0:1])

            with ExitStack() as sa:
                NPT = 4
                pt = [sb(sa, "pt%d" % i, [128, 1024], BF16) for i in range(NPT)]
                num = [sb(sa, "num%d" % i, [128, 512], F32) for i in range(2)]
                rden = [sb(sa, "rden%d" % i, [128, 512], F32) for i in range(2)]
                rdsw = [sb(sa, "rdsw%d" % i, [128, 512], F32) for i in range(2)]
                NSP = 3
                Sps = [pb_t[i] for i in range(NSP)]
                accp = [pb_t[3]]

                steps = [(qc, c, hh, ktp) for qc in range(4) for c in range(4) for hh in range(2) for ktp in range(9)]

                def emit_qk(i):
                    qc, c, hh, ktp = steps[i]
                    h = 2 * c + hh
                    s = i % NSP

                    def fn(e):
                        for j in range(2):
                            kt = 2 * ktp + j
                            ins = e.matmul(Sps[s][:, j * 512:(j + 1) * 512], lhsT=kT[h][:, kt * 128:(kt + 1) * 128],
                                           rhs=qT[h][:, qc * 512:(qc + 1) * 512], start=True, stop=True)
                        return ins

                    A("pe", fn, reads=[("kT", h, ci, hf) for ci in range(5) for hf in range(2)] + [("qT", h, qc)],
                      writes=[("bank", 2 * s), ("bank", 2 * s + 1)])

                deferred = []

                def emit_rest(i):
                    qc, c, hh, ktp = steps[i]
                    h = 2 * c + hh
                    s = i % NSP
                    p = i % NPT
                    pair = 0
                    rslot = (qc * 4 + c) % 2
                    A("act", lambda e: e.activation(out=pt[p][:], in_=Sps[s][:], func=AF.Exp, scale=SM_SCALE),
                      reads=[("bank", 2 * s), ("bank", 2 * s + 1)], writes=[("pt", p)])
                    acc = accp[pair][:, hh * 512:(hh + 1) * 512]
                    vbase = c * 192 + hh * 64

                    def fn(e):
                        for j in range(2):
                            kt = 2 * ktp + j
                            ins = e.matmul(acc, lhsT=Vaug[:, kt, vbase:vbase + 128], rhs=pt[p][:, j * 512:(j + 1) * 512],
                                           start=(kt == 0), stop=(kt == 17))
                        return ins

                    A("pe", fn, reads=[("pt", p), ("V", 2 * ktp), ("V", 2 * ktp + 1)], writes=[("bank", 6 + hh)])
                    if ktp == 8:
                        r = rslot
                        nlo, nhi = (0, 64) if hh == 0 else (64, 128)
                        dlo, dhi = (64, 128) if hh == 0 else (0, 64)
                        A("dve", lambda e: e.reciprocal(out=rden[r][dlo:dhi, :], in_=acc[dlo:dhi, :]),
                          reads=[("bank", 6 + hh)], writes=[("rden", r, hh)])
                        A("dve", lambda e: e.tensor_copy(out=num[r][nlo:nhi, :], in_=acc[nlo:nhi, :]),
                          reads=[("bank", 6 + hh)], writes=[("num", r, hh)])
                        A("sp", dma(rdsw[r][nlo:nhi, :], rden[r][dlo:dhi, :]), reads=[("rden", r, hh)],
                          writes=[("rdsw", r, hh)], dma=True)
                        if hh == 1:
                            deferred.append((i + 4, r, c, qc))

                def flush_deferred(i, force=False):
                    while deferred and (force or deferred[0][0] <= i):
                        _, r, c, qc = deferred.pop(0)
                        A("dve", lambda e, r=r, c=c, qc=qc: e.tensor_tensor(
                            out=attT[:, c, qc * 512:(qc + 1) * 512], in0=num[r][:], in1=rdsw[r][:], op=ALU.mult),
                          reads=[("num", r, 0), ("num", r, 1), ("rdsw", r, 0), ("rdsw", r, 1)],
                          writes=[("attT", c, qc)])

                emit_qk(0)
                emit_qk(1)
                for i in range(len(steps)):
                    if i + 2 < len(steps):
                        emit_qk(i + 2)
                    emit_rest(i)
                    flush_deferred(i)
                flush_deferred(0, force=True)
                dump("attT", attT[:], [128, 4, NTOK], BF16, [("attT", c, qc) for c in range(4) for qc in range(4)])
                P.barrier(scratch[:, 0:1])

        with ExitStack() as s2:
            ycv = sb(s2, "ycv", [128, 4, NTOK], BF16)
            merged = sb(s2, "merged", [128, 8, NTOK], BF16)
            woc_bf = sb(s2, "woc_bf", [128, 4, D], BF16)
            wom_bf = sb(s2, "wom_bf", [128, 4, D], BF16)
            wo_bf = sb(s2, "wo_bf", [128, 8, D], BF16)
            stg = [sb(s2, "stg%d" % i, [128, D], F32) for i in range(2)]
            stg_r = Ring([0, 1])
            Sps = [pb_t[i] for i in range(NSP)]
            accp = [pb_t[3]]
            wprep = []
            for (wd, wb, nm) in ((woc_d, woc_bf, "woc"), (wom_d, wom_bf, "wom")):
                for k in range(4):
                    def f(wd=wd, wb=wb, nm=nm, k=k):
                        s = stg_r.next()
                        A("sp", dma(stg[s][:], wd[k * 128:(k + 1) * 128, :]), writes=[("stg", s)], dma=True)
                        A("pool", lambda e: e.tensor_copy(out=wb[:, k, :], in_=stg[s][:]),
                          reads=[("stg", s)], writes=[(nm, k)])
                    wprep.append(f)
            for k in range(8):
                def f(k=k):
                    s = stg_r.next()
                    A("sp", dma(stg[s][:], wo_d[k * 128:(k + 1) * 128, :]), writes=[("stg", s)], dma=True)
                    A("pool", lambda e: e.tensor_tensor(out=wo_bf[:, k, :], in0=stg[s][:], in1=gate_bc[:], op=ALU.mult),
                      reads=[("stg", s)], writes=[("wo", k)])
                wprep.append(f)

            def _hook():
                if wprep:
                    wprep.pop(0)()

            ws_state["hook"] = _hook
            ws_state["cast"] = "act"

            with ExitStack() as sd1:
                sgm = [sb(sd1, "sgm%d" % i, [128, 512], F32) for i in range(2)]
                xcs = sb(sd1, "xcs", [128, NTOK], F32)
                u = sb(sd1, "u", [128, NTOK + 2], F32)
                v = sb(sd1, "v", [128, NTOK], F32)
                A("pool", lambda e: e.memset(u[:, 0:1], 0.0), writes=[("u", "l")])
                A("pool", lambda e: e.memset(u[:, NTOK + 1:NTOK + 2], 0.0), writes=[("u", "r")])
                si = 0
                d1_state = {"held": 0}

                def take1():
                    if d1_state["held"]:
                        rel_w(1)
                    d1_state["held"] = 1
                    return next_w()

                for c in range(4):
                    b = take1()
                    for n in range(4):
                        pa, pr = banks.next()
                        px_mm(b, n, pa, pr)
                        s = si % 2
                        si += 1
                        A("act", lambda e, pa=pa, s=s: e.activation(out=sgm[s][:], in_=pa, func=AF.Silu),
                          reads=[pr], writes=[("sgm", s)])
                        A("dve", lambda e, s=s, c=c, n=n: e.tensor_tensor(
                            out=attT[:, c, n * 512:(n + 1) * 512], in0=attT[:, c, n * 512:(n + 1) * 512], in1=sgm[s][:],
                            op=ALU.mult), reads=[("sgm", s), ("attT", c, n)], writes=[("attT", c, n)])
                for j in range(4):
                    b = take1()
                    for n in range(4):
                        pa, pr = banks.next()
                        px_mm(b, n, pa, pr)
                        A("act", lambda e, pa=pa, n=n: e.copy(out=xcs[:, n * 512:(n + 1) * 512], in_=pa),
                          reads=[pr], writes=[("xcs", n)])
                    b = take1()
                    for n in range(4):
                        pa, pr = banks.next()
                        px_mm(b, n, pa, pr)
                        A("dve", lambda e, pa=pa, n=n: e.tensor_tensor(
                            out=u[:, 1 + n * 512:1 + (n + 1) * 512], in0=pa, in1=xcs[:, n * 512:(n + 1) * 512],
                            op=ALU.mult), reads=[pr, ("xcs", n)], writes=[("u", n)])
                    ures = [("u", n) for n in range(4)] + [("u", "l"), ("u", "r")]
                    A("dve", lambda e, j=j: e.tensor_scalar_mul(out=v[:], in0=u[:, 0:NTOK],
                                                                scalar1=convw_sb[:, 3 * j:3 * j + 1]), reads=ures + ["convw"], writes=["v"])
                    A("dve", lambda e, j=j: e.scalar_tensor_tensor(out=v[:], in0=u[:, 1:NTOK + 1],
                                                                   scalar=convw_sb[:, 3 * j + 1:3 * j + 2], in1=v[:],
                                                                   op0=ALU.mult, op1=ALU.add),
                      reads=ures + ["convw", "v"], writes=["v"])
                    A("dve", lambda e, j=j: e.scalar_tensor_tensor(out=v[:], in0=u[:, 2:NTOK + 2],
                                                                   scalar=convw_sb[:, 3 * j + 2:3 * j + 3], in1=v[:],
                                                                   op0=ALU.mult, op1=ALU.add),
                      reads=ures + ["convw", "v"], writes=["v"])
                    b = take1()
                    for n in range(4):
                        pa, pr = banks.next()
                        px_mm(b, n, pa, pr)
                        A("dve", lambda e, pa=pa, n=n: e.tensor_tensor(
                            out=v[:, n * 512:(n + 1) * 512], in0=pa, in1=v[:, n * 512:(n + 1) * 512], op=ALU.mult),
                          reads=[pr, "v"], writes=[("v2", n)])
                    b = take1()
                    for n in range(4):
                        pa, pr = banks.next()
                        px_mm(b, n, pa, pr)
                        s = si % 2
                        si += 1
                        A("act", lambda e, pa=pa, s=s: e.activation(out=sgm[s][:], in_=pa, func=AF.Silu),
                          reads=[pr], writes=[("sgm", s)])
                        A("dve", lambda e, s=s, j=j, n=n: e.tensor_tensor(
                            out=ycv[:, j, n * 512:(n + 1) * 512], in0=sgm[s][:], in1=v[:, n * 512:(n + 1) * 512],
                            op=ALU.mult), reads=[("sgm", s), ("v2", n)], writes=[("ycv", j, n), "v"])
                rel_w(1)
                dump("attg", attT[:], [128, 4, NTOK], BF16, [("attT", c, qc) for c in range(4) for qc in range(4)])
                dump("ycv", ycv[:], [128, 4, NTOK], BF16, [("ycv", j, n) for j in range(4) for n in range(4)])
                P.barrier(scratch[:, 0:1])

            with ExitStack() as sd2:
                sgc = [sb(sd2, "sgc%d" % i, [128, 512], F32) for i in range(2)]
                sgl = [sb(sd2, "sgl%d" % i, [128, 512], F32) for i in range(2)]
                ta = [sb(sd2, "ta%d" % i, [128, 512], F32) for i in range(2)]
                tb = [sb(sd2, "tb%d" % i, [128, 512], F32) for i in range(2)]
                si = 0
                for i in range(8):
                    bc_ = next_w()
                    bm_ = next_w()
                    for n in range(4):
                        s = si % 2
                        si += 1
                        pa, pr = banks.next()
                        px_mm(bc_, n, pa, pr)
                        A("act", lambda e, pa=pa, s=s: e.activation(out=sgc[s][:], in_=pa, func=AF.Sigmoid),
                          reads=[pr], writes=[("sgc", s)])
                        pa, pr = banks.next()
                        px_mm(bm_, n, pa, pr)
                        A("act", lambda e, pa=pa, s=s: e.activation(out=sgl[s][:], in_=pa, func=AF.Sigmoid),
                          reads=[pr], writes=[("sgl", s)])
                        pa, pr = banks.next()

                        def fn(e, pa=pa, i=i, n=n):
                            for j in range(4):
                                ins = e.matmul(pa, lhsT=woc_bf[:, j, i * 128:(i + 1) * 128],
                                               rhs=ycv[:, j, n * 512:(n + 1) * 512], start=(j == 0), stop=(j == 3))
                            return ins

                        A("pe", fn, reads=[("woc", j) for j in range(4)] + [("ycv", j, n) for j in range(4)],
                          writes=[pr])
                        A("dve", lambda e, pa=pa, s=s: e.tensor_tensor(out=ta[s][:], in0=pa, in1=sgc[s][:],
                                                                       op=ALU.mult),
                          reads=[pr, ("sgc", s)], writes=[("ta", s)])
                        pa, pr = banks.next()

                        def fn(e, pa=pa, i=i, n=n):
                            for j in range(4):
                                ins = e.matmul(pa, lhsT=wom_bf[:, j, i * 128:(i + 1) * 128],
                                               rhs=attT[:, j, n * 512:(n + 1) * 512], start=(j == 0), stop=(j == 3))
                            return ins

                        A("pe", fn, reads=[("wom", j) for j in range(4)] + [("attT", j, n) for j in range(4)],
                          writes=[pr])
                        A("dve", lambda e, pa=pa, s=s: e.tensor_tensor(out=tb[s][:], in0=pa, in1=sgl[s][:],
                                                                       op=ALU.mult),
                          reads=[pr, ("sgl", s)], writes=[("tb", s)])
                        A("dve", lambda e, s=s, i=i, n=n: e.tensor_tensor(
                            out=merged[:, i, n * 512:(n + 1) * 512], in0=ta[s][:], in1=tb[s][:], op=ALU.add),
                          reads=[("ta", s), ("tb", s)], writes=[("merged", i, n)])
                    rel_w(2)
                dump("merged", merged[:], [128, 8, NTOK], BF16, [("merged", i, n) for i in range(8) for n in range(4)])
                P.barrier(scratch[:, 0:1])

            with ExitStack() as sd3:
                NS = 4
                xs2 = [sb(sd3, "xs2_%d" % i, [128, D], F32) for i in range(NS)]
                rr = [sb(sd3, "rr%d" % i, [128, D], F32) for i in range(NS)]
                lng = sb(sd3, "lng", [128, D], F32)
                lnb = sb(sd3, "lnb", [128, D], F32)
                st6 = [sb(sd3, "st6_%d" % i, [128, 12], F32) for i in range(NS)]
                mv = [sb(sd3, "mv%d" % i, [128, 8], F32) for i in range(NS)]
                junk = stg[0]
                A("sp", dma(lng[:], lng_d.partition_broadcast(128)), writes=["lng"], dma=True)
                A("sp", dma(lnb[:], lnb_d.partition_broadcast(128)), writes=["lnb"], dma=True)
                out_ops = []

                def load_x(tt):
                    A("sp", dma(xs2[tt % NS][:], x_d[tt * 128:(tt + 1) * 128, :]), writes=[("xs2", tt % NS)], dma=True)

                for tt in range(NS):
                    load_x(tt)

                def early(tt):
                    s = tt % NS
                    pa = pb_t[tt % 4]
                    pr = ("bank", 2 * (tt % 4))
                    pr2 = ("bank", 2 * (tt % 4) + 1)

                    def fn(e):
                        for hf in range(2):
                            for kc in range(8):
                                ins = e.matmul(pa[:, hf * 512:(hf + 1) * 512], lhsT=merged[:, kc, tt * 128:(tt + 1) * 128],
                                               rhs=wo_bf[:, kc, hf * 512:(hf + 1) * 512], start=(kc == 0), stop=(kc == 7))
                        return ins

                    A("pe", fn, reads=[("merged", i, tt // 4) for i in range(8)] + [("wo", k) for k in range(8)],
                      writes=[pr, pr2])
                    A("dve", lambda e: e.scalar_tensor_tensor(out=rr[s][:], in0=xs2[s][:], scalar=ALPHA,
                                                              in1=pa[:], op0=ALU.mult, op1=ALU.add,
                                                              accum_out=mv[s][:, 0:1]),
                      reads=[pr, pr2, ("xs2", s)], writes=[("rr", s), ("mv", s, 0)])
                    if tt + NS < 16:
                        load_x(tt + NS)
                    A("act", lambda e: e.activation(out=junk[:], in_=rr[s][:], func=AF.Square,
                                                    accum_out=mv[s][:, 1:2]),
                      reads=[("rr", s)], writes=["junk", ("mv", s, 1)])
                    A("dve", lambda e: e.tensor_scalar_mul(out=mv[s][:, 4:5], in0=mv[s][:, 0:1], scalar1=1.0 / D),
                      reads=[("mv", s, 0)], writes=[("mv", s, 4)])
                    A("dve", lambda e: e.scalar_tensor_tensor(out=mv[s][:, 5:6], in0=mv[s][:, 4:5], scalar=-1.0,
                                                              in1=mv[s][:, 4:5], op0=ALU.mult, op1=ALU.mult),
                      reads=[("mv", s, 4)], writes=[("mv", s, 5)])
                    A("dve", lambda e: e.scalar_tensor_tensor(out=mv[s][:, 6:7], in0=mv[s][:, 1:2], scalar=1.0 / D,
                                                              in1=mv[s][:, 5:6], op0=ALU.mult, op1=ALU.add),
                      reads=[("mv", s, 1), ("mv", s, 5)], writes=[("mv", s, 6)])
                    A("act", lambda e: e.activation(out=mv[s][:, 2:3], in_=mv[s][:, 6:7], func=AF.Sqrt,
                                                    bias=consts[:, 1:2], scale=1.0),
                      reads=[("mv", s, 6), "consts"], writes=[("mv2", s)])
                    A("dve", lambda e: e.reciprocal(out=mv[s][:, 2:3], in_=mv[s][:, 2:3]),
                      reads=[("mv2", s)], writes=[("mv2", s)])
                    A("dve", lambda e: e.scalar_tensor_tensor(out=mv[s][:, 3:4], in0=mv[s][:, 4:5], scalar=-1.0,
                                                              in1=mv[s][:, 2:3], op0=ALU.mult, op1=ALU.mult),
                      reads=[("mv", s, 4), ("mv2", s)], writes=[("mv3", s)])
                    A("act", lambda e: e.activation(out=rr[s][:], in_=rr[s][:], func=AF.Identity,
                                                    bias=mv[s][:, 3:4], scale=mv[s][:, 2:3]),
                      reads=[("rr", s), ("mv2", s), ("mv3", s)], writes=[("rr", s)])

                def late(tt):
                    s = tt % NS
                    A("pool", lambda e: e.tensor_tensor(out=rr[s][:], in0=rr[s][:], in1=lng[:], op=ALU.mult),
                      reads=[("rr", s), "lng"], writes=[("rr", s)])
                    A("dve",
                      lambda e: e.tensor_tensor(out=rr[s][:], in0=rr[s][:], in1=lnb[:], op=ALU.add),
                      reads=[("rr", s), "lnb"], writes=[("rr", s)])
                    out_ops.append(A("sp", dma(out_d[tt * 128:(tt + 1) * 128, :], rr[s][:]), reads=[("rr", s)],
                                     dma=True))

                SKEW = 2
                for tt in range(16 + SKEW):
                    if tt < 16:
                        early(tt)
                    if tt - SKEW >= 0:
                        late(tt - SKEW)
                A("sp", None, deps=out_ops + list(dbg_outs.values()))
    return nc


_CACHE = {}


def _rope_tables():
    grid_w = 64
    t = np.arange(NTOK)
    row = (t // grid_w).astype(np.float32)
    col = (t % grid_w).astype(np.float32)
    inv = (np.float32(10000.0) ** (-np.arange(0, 16, 2, dtype=np.float32) / np.float32(16))).astype(np.float32)
    ar = row[:, None] * inv[None, :]
    ac = col[:, None] * inv[None, :]
    cr, sr, cc, sc = np.cos(ar), np.sin(ar), np.cos(ac), np.sin(ac)
    C = np.concatenate([cr, cr, cc, cc], axis=1).astype(np.float32)
    S = np.concatenate([-sr, sr, -sc, sc], axis=1).astype(np.float32)
    return C, S


def _swap32(a):
    return np.concatenate([a[..., 8:16], a[..., 0:8], a[..., 24:32], a[..., 16:24]], axis=-1)


def kernel(x, c, ctx, c_ctx, w_ada, b_ada, w_in, conv_w, q_norm_g, w_uq, kv_norm_g, w_ukv,
           w_out_conv, w_out_mla, w_o, ln_g, ln_b, _debug=()):
    f = np.float32
    x = np.asarray(x, f); c = np.asarray(c, f); ctx = np.asarray(ctx, f); c_ctx = np.asarray(c_ctx, f)
    w_ada = np.ascontiguousarray(np.asarray(w_ada, f)[0]); b_ada = np.asarray(b_ada, f)[0]
    w_in = np.ascontiguousarray(np.asarray(w_in, f)[0]); conv_w = np.asarray(conv_w, f)[0]
    q_norm_g = np.asarray(q_norm_g, f)[0]; w_uq = np.asarray(w_uq, f)[0]
    kv_norm_g = np.asarray(kv_norm_g, f)[0]; w_ukv = np.asarray(w_ukv, f)[0]
    w_out_conv = np.ascontiguousarray(np.asarray(w_out_conv, f)[0])
    w_out_mla = np.ascontiguousarray(np.asarray(w_out_mla, f)[0])
    w_o = np.ascontiguousarray(np.asarray(w_o, f)[0]); ln_g = np.asarray(ln_g, f)[0]; ln_b = np.asarray(ln_b, f)[0]
    B = x.shape[0]

    C, S = _rope_tables()
    tabq = np.ascontiguousarray(np.concatenate([np.ones((64, NTOK), f), C.T, S.T], axis=0))
    Ck = np.concatenate([np.ones((32, NCTX), f), C.T], axis=1)
    Sk = np.concatenate([np.zeros((32, NCTX), f), S.T], axis=1)
    tabk = np.ascontiguousarray(np.stack([np.concatenate([Ck, Ck], 0), np.concatenate([Sk, Sk], 0)], 0))
    kr = w_in[:, 2432:2464]
    krs = _swap32(kr)
    w_kr = np.ascontiguousarray(np.concatenate([kr] * 4 + [krs] * 4, axis=1))
    wq = w_uq.reshape(256, 8, 96)
    w_uq_aug = np.ascontiguousarray(
        np.concatenate([wq[:, :, 0:64], wq[:, :, 64:96], _swap32(wq[:, :, 64:96])], axis=2).reshape(256, 1024))
    wkv = w_ukv.reshape(128, 8, 128)
    w_ukv_r = np.ascontiguousarray(
        np.concatenate([wkv[:, :, 0:64].reshape(128, 512), wkv[:, :, 64:128].reshape(128, 512)], axis=1))
    bada = b_ada[0:2048].reshape(16, 128).T
    bada2 = np.ascontiguousarray(np.repeat(bada[:, :, None], 2, axis=2).reshape(128, 32))
    badag = np.ascontiguousarray(b_ada[2048:3072])
    convw = np.ascontiguousarray(conv_w.reshape(3, 4, 128).transpose(2, 1, 0).reshape(128, 12))
    gq = np.ascontiguousarray(q_norm_g.reshape(2, 128).T)
    gkv = np.ascontiguousarray(kv_norm_g.reshape(1, 128).T)
    cctx_cols = c_ctx.reshape(8, 128).T

    key = tuple(sorted(_debug))
    if key not in _CACHE:
        _CACHE[key] = build_program(debug=_debug)
    nc = _CACHE[key]
    in_maps = []
    for b in range(B):
        cc = np.stack([c[b].reshape(8, 128).T, cctx_cols], axis=2).reshape(128, 16)
        in_maps.append({
            "x": np.ascontiguousarray(x[b]), "ctx": np.ascontiguousarray(ctx[b]), "cc": np.ascontiguousarray(cc),
            "w_ada": w_ada, "bada": bada2, "badag": badag, "w_in": w_in, "w_kr": w_kr, "convw": convw,
            "gq": gq, "gkv": gkv, "w_uq": w_uq_aug, "w_ukv": w_ukv_r, "w_oc": w_out_conv, "w_om": w_out_mla,
            "w_o": w_o, "ln_g": ln_g, "ln_b": ln_b, "tabq": tabq, "tabk": tabk,
        })
    res = run_bass_kernel_spmd(nc, in_maps, core_ids=list(range(B)))
    out = np.stack([np.asarray(r["out"], f) for r in res.results], axis=0)
    if _debug:
        return out, res.results
    return out
```

```python
import numpy as np
import concourse.bass as bass
import concourse.mybir as mybir
from concourse.bass_utils import run_bass_kernel_spmd

F32 = mybir.dt.float32
BF16 = mybir.dt.bfloat16
AF = mybir.ActivationFunctionType
ALU = mybir.AluOpType

D = 1024
NTOK = 2048
NCTX = 256
NKEY = NTOK + NCTX
NIN = 5024
LN_EPS = 1e-5
RMS_EPS = 1e-6
ALPHA = 2.0 ** 0.25
SM_SCALE = 96.0 ** -0.5
ND = 32


class Op:
    __slots__ = ("eng", "dma", "sem", "val", "idx", "key")


class Prog:
    ENGS = ("sp", "act", "pool", "pe", "dve")

    def __init__(self, nc, sems, dma_sems):
        self.nc = nc
        self.eobj = {"sp": nc.sync, "act": nc.scalar, "pool": nc.gpsimd, "pe": nc.tensor, "dve": nc.vector}
        self.sems = sems
        self.dma_sems = dma_sems
        self.cnt = {e: 0 for e in self.ENGS}
        self.nops = {e: 0 for e in self.ENGS}
        self.known = {e: {} for e in self.ENGS}
        self.lastw = {}
        self.readers = {}
        self.barrier_op = None
        self.dmas = []
        self.pending_dmas = []
        self.last_op = {}

    def _needs_wait(self, eng, is_dma, idx, d):
        if d.dma:
            return True
        if d.eng != eng:
            return True
        if is_dma:
            return True
        if eng == "pe":
            return False
        return (idx - d.idx) <= 2

    def add(self, eng, fn, reads=(), writes=(), dma=False, deps=()):
        deps = set(deps)
        if self.barrier_op is not None and eng != "pe":
            deps.add(self.barrier_op)
        for r in reads:
            deps |= self.lastw.get(r, set())
        for w in writes:
            deps |= self.lastw.get(w, set())
            deps |= self.readers.get(w, set())
        op = Op()
        op.eng = eng
        op.dma = dma
        op.idx = self.nops[eng]
        self.nops[eng] += 1
        if dma:
            i = len(self.dmas)
            if i >= ND:
                deps.add(self.dmas[i - ND])
            op.sem = self.dma_sems[i % ND]
            op.val = 16 * (i // ND + 1)
            op.key = ("d", i % ND)
            self.dmas.append(op)
            self.pending_dmas.append(op)
        else:
            op.sem = self.sems[eng]
            op.key = ("e", eng)
            op.val = None
        e = self.eobj[eng]
        need = {}
        for d in deps:
            if not self._needs_wait(eng, dma, op.idx, d):
                continue
            if need.get(d.key, (None, 0))[1] < d.val:
                need[d.key] = (d.sem, d.val)
        kn = self.known[eng]
        for k, (sem, val) in need.items():
            if kn.get(k, 0) < val:
                e.wait_ge(sem, val)
                kn[k] = val
        if fn is not None:
            ins = fn(e)
            if dma:
                ins.then_inc(op.sem, 16)
            else:
                self.cnt[eng] += 1
                op.val = self.cnt[eng]
                ins.then_inc(op.sem, 1)
        else:
            op.val = self.cnt[eng]
        for r in reads:
            self.readers.setdefault(r, set()).add(op)
        for w in writes:
            if self.readers.get(w):
                self.lastw[w] = {op}
                self.readers[w] = set()
            else:
                self.lastw.setdefault(w, set()).add(op)
        self.last_op[eng] = op
        return op

    def barrier(self, scratch_ap):
        deps = set(self.last_op.values()) | set(self.pending_dmas)
        b = self.add("dve", lambda e: e.memset(scratch_ap, 0.0), deps=deps)
        self.barrier_op = b
        self.pending_dmas = []
        return b


class Ring:
    def __init__(self, items):
        self.items = items
        self.i = 0

    def next(self):
        it = self.items[self.i % len(self.items)]
        self.i += 1
        return it


def build_program(debug=()):
    nc = bass.Bass("TRN2", target_bir_lowering=False)

    def din(name, shape):
        return nc.dram_tensor(name, list(shape), F32, kind="ExternalInput").ap()

    x_d = din("x", [NTOK, D])
    ctx_d = din("ctx", [NCTX, D])
    cc_d = din("cc", [128, 16])
    wada_d = din("w_ada", [D, 3 * D])
    bada_d = din("bada", [128, 32])
    badag_d = din("badag", [D])
    win_d = din("w_in", [D, NIN])
    wkr_d = din("w_kr", [D, 256])
    convw_d = din("convw", [128, 12])
    gq_d = din("gq", [128, 2])
    gkv_d = din("gkv", [128, 1])
    wuq_d = din("w_uq", [256, 1024])
    wukv_d = din("w_ukv", [128, 1024])
    woc_d = din("w_oc", [512, D])
    wom_d = din("w_om", [512, D])
    wo_d = din("w_o", [D, D])
    lng_d = din("ln_g", [D])
    lnb_d = din("ln_b", [D])
    tabq_d = din("tabq", [128, NTOK])
    tabk_d = din("tabk", [2, 64, NKEY])
    out_d = nc.dram_tensor("out", [NTOK, D], F32, kind="ExternalOutput").ap()

    dbg_outs = {}

    from contextlib import ExitStack

    with ExitStack() as top:
        sems = {e: top.enter_context(nc.semaphore("s_" + e)) for e in Prog.ENGS}
        dma_sems = [top.enter_context(nc.semaphore("d%d" % i)) for i in range(ND)]
        P = Prog(nc, sems, dma_sems)
        A = P.add

        def sb(stack, name, shape, dt):
            return stack.enter_context(nc.sbuf_tensor(name, list(shape), dt))

        def ps(stack, name, shape, dt=F32):
            return stack.enter_context(nc.psum_tensor(name, list(shape), dt))

        def dma(out, in_):
            return lambda e: e.dma_start(out=out, in_=in_)

        def dump(name, ap, shape, dt, reads):
            if name not in debug:
                return
            t = nc.dram_tensor("dbg_" + name, list(shape), dt, kind="ExternalOutput").ap()
            dbg_outs[name] = A("sp", dma(t, ap), reads=reads, dma=True)

        hxT = sb(top, "hxT", [128, 8, NTOK], BF16)
        hcT = sb(top, "hcT", [128, 8, NCTX], BF16)
        attT = sb(top, "attT", [128, 4, NTOK], BF16)
        ident = sb(top, "ident", [128, 128], BF16)
        ones_bf = sb(top, "ones_bf", [128, 128], BF16)
        cc_sb = sb(top, "cc_sb", [128, 16], F32)
        cs = sb(top, "cs", [128, 16], F32)
        mod = sb(top, "mod", [128, 32], F32)
        bada_sb = sb(top, "bada_sb", [128, 32], F32)
        consts = sb(top, "consts", [128, 8], F32)
        scratch = sb(top, "scratch", [128, 8], F32)
        convw_sb = sb(top, "convw_sb", [128, 12], F32)
        gq_sb = sb(top, "gq_sb", [128, 2], F32)
        gkv_sb = sb(top, "gkv_sb", [128, 1], F32)
        wst = [sb(top, "wst%d" % i, [128, 8, 128], F32) for i in range(3)]
        wbf = [sb(top, "wbf%d" % i, [128, 8, 128], BF16) for i in range(5)]
        wst_r = Ring([0, 1, 2])
        wbf_r = Ring([0, 1, 2, 3, 4])
        gate_bc = sb(top, "gate_bc", [128, D], F32)
        pb_t = [ps(top, "pb%d" % i, [128, 1024]) for i in range(4)]

        def bank_f32(i):
            return pb_t[i // 2][:, (i % 2) * 512:(i % 2) * 512 + 512]

        def bank_bf16(i):
            return bank_f32(i).bitcast(BF16)

        all_banks = [(bank_f32(i), ("bank", i)) for i in range(8)]

        A("pool", lambda e: e.memset(ident[:], 0.0), writes=["ident"])
        A("pool", lambda e: e.affine_select(out=ident[:], in_=ident[:], pattern=[[-1, 128]],
                                            compare_op=ALU.not_equal, fill=1.0, base=0, channel_multiplier=1),
          reads=["ident"], writes=["ident"])
        A("pool", lambda e: e.memset(ones_bf[:], 1.0), writes=["ones"])
        A("pool", lambda e: e.memset(consts[:, 0:1], RMS_EPS), writes=["consts"])
        A("pool", lambda e: e.memset(consts[:, 1:2], LN_EPS), writes=["consts"])
        A("sp", dma(cc_sb[:], cc_d), writes=["cc"], dma=True)
        A("sp", dma(bada_sb[:], bada_d), writes=["bada"], dma=True)
        A("sp", dma(convw_sb[:], convw_d), writes=["convw"], dma=True)
        A("sp", dma(gq_sb[:], gq_d), writes=["gq"], dma=True)
        A("sp", dma(gkv_sb[:], gkv_d), writes=["gkv"], dma=True)
        A("act", lambda e: e.activation(out=cs[:], in_=cc_sb[:], func=AF.Silu), reads=["cc"], writes=["cs"])

        wsrcs = [win_d[:, 2304:2432], wkr_d[:, 0:128], wkr_d[:, 128:256], win_d[:, 2048:2176], win_d[:, 2176:2304]]
        wsrcs += [win_d[:, 2464 + c * 128:2464 + (c + 1) * 128] for c in range(4)]
        for j in range(4):
            wsrcs += [win_d[:, o + j * 128:o + (j + 1) * 128] for o in (0, 1024, 512, 1536)]
        for i in range(8):
            wsrcs += [win_d[:, o + i * 128:o + (i + 1) * 128] for o in (2976, 4000)]
        ws_state = {"issued": 0, "taken": 0, "released": 0, "slots": [], "cast": "act", "hook": None}
        AHEAD = 3

        def _issue_w():
            k = ws_state["issued"]
            src = wsrcs[k]
            ws_state["issued"] += 1
            s = wst_r.next()
            b = wbf_r.next()
            A("sp", dma(wst[s][:], src.rearrange("(k p) m -> p k m", p=128)), writes=[("wst", s)], dma=True)
            ce = ws_state["cast"]
            if ce == "act":
                A("act", lambda e: e.copy(out=wbf[b][:], in_=wst[s][:]), reads=[("wst", s)], writes=[("wbf", b)])
            else:
                A(ce, lambda e: e.tensor_copy(out=wbf[b][:], in_=wst[s][:]), reads=[("wst", s)], writes=[("wbf", b)])
            ws_state["slots"].append(b)
            if ws_state["hook"] is not None:
                ws_state["hook"]()

        def ws_prefetch(n_ahead=AHEAD):
            while (ws_state["issued"] < len(wsrcs) and ws_state["issued"] - ws_state["taken"] < n_ahead
                   and ws_state["issued"] - ws_state["released"] < len(wbf)):
                _issue_w()

        def next_w():
            if ws_state["issued"] == ws_state["taken"]:
                assert ws_state["issued"] - ws_state["released"] < len(wbf)
                _issue_w()
            b = ws_state["slots"].pop(0)
            ws_state["taken"] += 1
            ws_prefetch()
            return b

        def rel_w(n=1):
            ws_state["released"] += n
            ws_prefetch()

        def px_mm(b, n, ps_ap, ps_res, is_ctx=False, m0=0, m1=128):
            if is_ctx:
                rhs = [hcT[:, kc, :] for kc in range(8)]
                reads = [("hcT", kc) for kc in range(8)]
            else:
                rhs = [hxT[:, kc, n * 512:(n + 1) * 512] for kc in range(8)]
                reads = [("hxT", kc, n // 2) for kc in range(8)]

            def fn(e):
                for kc in range(8):
                    ins = e.matmul(ps_ap, lhsT=wbf[b][:, kc, m0:m1], rhs=rhs[kc], start=(kc == 0), stop=(kc == 7))
                return ins

            return A("pe", fn, reads=reads + [("wbf", b)], writes=[ps_res])

        with ExitStack() as s0:
            NXS = 8
            wa = [sb(s0, "wa%d" % i, [128, 2048], F32) for i in range(2)]
            wab = [sb(s0, "wab%d" % i, [128, 2048], BF16) for i in range(2)]
            xs = [sb(s0, "xs%d" % i, [128, D], F32) for i in range(NXS)]
            xbf = [sb(s0, "xbf%d" % i, [128, D], BF16) for i in range(NXS)]
            cs_bf = sb(s0, "cs_bf", [128, 16], BF16)
            modps = bank_f32(7)

            A("dve", lambda e: e.tensor_copy(out=cs_bf[:], in_=cs[:]), reads=["cs"], writes=["cs_bf"])

            EVQ = {0: (0, 0), 2: (0, 1), 4: (0, 2), 1: (1, 0), 3: (1, 1), 5: (1, 2), 6: (1, 3), 7: (1, 4)}
            tiles = [(x_d[j * 128:(j + 1) * 128, :], hxT, j * 128, 0) for j in range(16)]
            tiles += [(ctx_d[t * 128:(t + 1) * 128, :], hcT, t * 128, 1) for t in range(2)]
            NT = len(tiles)

            def load_tile(j):
                s = j % NXS
                A("sp", dma(xs[s][:], tiles[j][0]), writes=[("xs", s)], dma=True)

            def cast_tile(j):
                s = j % NXS
                if j % 2 == 0:
                    A("act", lambda e: e.copy(out=xbf[s][:], in_=xs[s][:]), reads=[("xs", s)], writes=[("xbf", s)])
                else:
                    A("dve", lambda e: e.tensor_copy(out=xbf[s][:], in_=xs[s][:]),
                      reads=[("xs", s)], writes=[("xbf", s)])

            groups = [list(range(4 * g, 4 * g + 4)) for g in range(4)] + [[16, 17]]
            tpb = Ring([(bank_bf16(i), ("bank", i)) for i in range(6)])

            def do_group(g):
                tl = groups[g]
                _, dstT, c0, mj = tiles[tl[0]]
                W = 128 * len(tl)
                for kc in range(8):
                    bank, bres = tpb.next()

                    def fn(e, bank=bank, kc=kc):
                        for i, j in enumerate(tl):
                            ins = e.transpose(bank[:, i * 128:(i + 1) * 128],
                                              xbf[j % NXS][:, kc * 128:(kc + 1) * 128], ident[:])
                        return ins

                    A("pe", fn, reads=[("xbf", j % NXS) for j in tl] + ["ident"], writes=[bres])
                    dst = dstT[:, kc, c0:c0 + W]
                    sc_ap = mod[:, 16 + 2 * kc + mj:16 + 2 * kc + mj + 1]
                    sh_ap = mod[:, 2 * kc + mj:2 * kc + mj + 1]
                    res = ("hxT", kc, c0 // 1024) if mj == 0 else ("hcT", kc)
                    if kc % 2 == 0:
                        A("act", lambda e, dst=dst, bank=bank, sc_ap=sc_ap, sh_ap=sh_ap: e.activation(
                            out=dst, in_=bank[:, 0:W], func=AF.Identity, bias=sh_ap, scale=sc_ap),
                          reads=[bres, "mod"], writes=[res])
                    else:
                        A("dve", lambda e, dst=dst, bank=bank, sc_ap=sc_ap, sh_ap=sh_ap: e.tensor_scalar(
                            out=dst, in0=bank[:, 0:W], scalar1=sc_ap, scalar2=sh_ap, op0=ALU.mult, op1=ALU.add),
                          reads=[bres, "mod"], writes=[res])

            for kc in range(8):
                s = kc % 2
                A("sp", dma(wa[s][:], wada_d[kc * 128:(kc + 1) * 128, 0:2048]), writes=[("wa", s)], dma=True)
                A("dve", lambda e, s=s: e.tensor_copy(out=wab[s][:], in_=wa[s][:]), reads=[("wa", s)],
                  writes=[("wab", s)])

                def fn(e, kc=kc, s=s):
                    for m in range(16):
                        ins = e.matmul(modps[:, 2 * m:2 * m + 2], lhsT=wab[s][:, m * 128:(m + 1) * 128],
                                       rhs=cs_bf[:, 2 * kc:2 * kc + 2], start=(kc == 0 and m == 0), stop=(kc == 7),
                                       skip_group_check=True)
                    return ins

                A("pe", fn, reads=[("wab", s), "cs_bf"], writes=[("bank", 7)])
            for j in range(NXS):
                load_tile(j)
                if j == 3:
                    ws_prefetch()
            A("dve", lambda e: e.tensor_tensor(out=mod[:], in0=modps[:, 0:32], in1=bada_sb[:], op=ALU.add),
              reads=[("bank", 7), "bada"], writes=["mod"])
            A("dve", lambda e: e.tensor_scalar_add(out=mod[:, 16:32], in0=mod[:, 16:32], scalar1=1.0),
              reads=["mod"], writes=["mod"])
            for j in range(NXS):
                cast_tile(j)
                load_tile(j + NXS)
            for g in range(len(groups)):
                do_group(g)
                if g + 2 < len(groups):
                    for j in groups[g + 2]:
                        cast_tile(j)
                        if j + NXS < NT:
                            load_tile(j + NXS)
            if "stopA" in debug:
                dump("hxT", hxT[:], [128, 8, NTOK], BF16, [("hxT", kc, h) for kc in range(8) for h in range(2)])
                A("sp", None, deps=list(dbg_outs.values()))
                return nc
            dump("hxT", hxT[:], [128, 8, NTOK], BF16, [("hxT", kc, h) for kc in range(8) for h in range(2)])
            dump("hcT", hcT[:], [128, 8, NCTX], BF16, [("hcT", kc) for kc in range(8)])
            dump("mod", mod[:], [128, 32], F32, ["mod"])
            P.barrier(scratch[:, 0:1])

        with ExitStack() as s1:
            kT = [sb(s1, "kT%d" % h, [128, NKEY], BF16) for h in range(8)]
            Vaug = sb(s1, "Vaug", [128, 18, 768], BF16)
            wuq_bf = sb(s1, "wuq_bf", [128, 2, 1024], BF16)
            kchunks = [(0, 256, True, 0)] + [(256 + 512 * n, 512, False, n) for n in range(4)]

            with ExitStack() as skv:
                ckvnT = sb(skv, "ckvnT", [128, NKEY], BF16)
                krot = sb(skv, "krot", [128, NKEY], BF16)
                Ck = sb(skv, "Ck", [128, NKEY], F32)
                Sk = sb(skv, "Sk", [128, NKEY], F32)
                t1 = [sb(skv, "t1_0", [128, 512], F32)] * 2
                t2 = [sb(skv, "t2_0", [128, 512], F32)] * 2
                sq = [sb(skv, "sq%d" % i, [128, 512], BF16) for i in range(2)]
                rs = [sb(skv, "rs%d" % i, [128, 512], F32) for i in range(2)]
                wukv_st = sb(skv, "wukv_st", [128, 1024], F32)
                wukv_bf = sb(skv, "wukv_bf", [128, 1024], BF16)
                banks = Ring(list(all_banks))

                ws_state["cast"] = "pool"
                b = next_w()
                bA = next_w()
                bB = next_w()
                A("sp", dma(Ck[64:128, :], tabk_d[0]), writes=["Ck"], dma=True)
                A("sp", dma(Sk[64:128, :], tabk_d[1]), writes=["Sk"], dma=True)
                A("sp", dma(wukv_st[:], wukv_d), writes=["wukv_st"], dma=True)

                csrep = sb(skv, "csrep", [128, 8, 128], F32)
                wag = [sb(skv, "wag%d" % i, [128, D], F32) for i in range(2)]
                bgate = sb(skv, "bgate", [128, D], F32)
                A("sp", dma(bgate[:], badag_d.partition_broadcast(128)), writes=["bgate"], dma=True)
                A("pool", lambda e: e.memset(csrep[:], 1.0), writes=[("csrep", kc) for kc in range(8)])
                def gate_prep_dve():
                    for kc in range(8):
                        A("dve", lambda e, kc=kc: e.tensor_scalar_mul(out=csrep[:, kc, :], in0=csrep[:, kc, :],
                                                                      scalar1=cs[:, 2 * kc:2 * kc + 1]),
                          reads=["cs", ("csrep", kc)], writes=[("csrep", kc)])

                allb = banks.items
                ring6 = Ring(allb[0:6])
                gpa = allb[6:8]

                def gate_dma(kc):
                    s = kc % 2
                    A("sp", dma(wag[s][:], wada_d[kc * 128:(kc + 1) * 128, 2048:3072]), writes=[("wag", s)], dma=True)

                def gate_step(kc):
                    s = kc % 2

                    def fn(e):
                        for hf in range(2):
                            ins = e.matmul(gpa[hf][0], lhsT=csrep[:, kc, :], rhs=wag[s][:, hf * 512:(hf + 1) * 512],
                                           start=(kc == 0), stop=(kc == 7))
                        return ins

                    A("pe", fn, reads=[("csrep", kc), ("wag", s)], writes=[gpa[0][1], gpa[1][1]])
                    if kc + 2 < 8:
                        gate_dma(kc + 2)

                gate_dma(0)
                gate_dma(1)
                gate_sched = {0: [], 1: [0, 1], 2: [2, 3], 3: [4, 5], 4: [6, 7]}

                for g in range(6):
                    A("pool", lambda e, g=g: e.memset(
                        Vaug[:, 3 * g:3 * g + 3, :].rearrange("p k (c s d) -> p k c s d", c=4, s=3)[:, :, :, 1, :], 1.0),
                      writes=[("V", kt) for kt in range(3 * g, 3 * g + 3)])
                for ci, (k0, W, isc, n) in enumerate(kchunks):
                    s = ci % 2
                    pa, pr = ring6.next()
                    px_mm(b, n, pa[:, :W], pr, is_ctx=isc)
                    A("act", lambda e, pa=pa, s=s, W=W: e.activation(out=sq[s][:, :W], in_=pa[:, :W], func=AF.Square),
                      reads=[pr], writes=[("sq", s)])
                    paA, prA = ring6.next()
                    px_mm(bA, n, paA[:, :W], prA, is_ctx=isc)
                    paB, prB = ring6.next()
                    px_mm(bB, n, paB[:, :W], prB, is_ctx=isc)
                    pa2, pr2 = ring6.next()
                    A("pe", lambda e, pa2=pa2, s=s, W=W: e.matmul(pa2[:, :W], lhsT=ones_bf[:], rhs=sq[s][:, :W],
                                                                  start=True, stop=True),
                      reads=[("sq", s), "ones"], writes=[pr2])
                    A("act", lambda e, pa2=pa2, s=s, W=W: e.activation(out=rs[s][:, :W], in_=pa2[:, :W], func=AF.Ln,
                                                                       bias=consts[:, 0:1], scale=1.0 / 128.0),
                      reads=[pr2, "consts"], writes=[("rs", s)])
                    A("act", lambda e, s=s, W=W: e.activation(out=rs[s][:, :W], in_=rs[s][:, :W], func=AF.Exp,
                                                              scale=-0.5),
                      reads=[("rs", s)], writes=[("rs", s)])
                    A("dve", lambda e, paA=paA, s=s, W=W, k0=k0: e.tensor_tensor(
                        out=t1[s][64:128, :W], in0=paA[64:128, :W], in1=Ck[64:128, k0:k0 + W], op=ALU.mult),
                      reads=[prA, "Ck"], writes=["t1"])
                    A("dve", lambda e, paB=paB, s=s, W=W, k0=k0: e.tensor_tensor(
                        out=t2[s][64:128, :W], in0=paB[64:128, :W], in1=Sk[64:128, k0:k0 + W], op=ALU.mult),
                      reads=[prB, "Sk"], writes=["t2"])
                    A("dve", lambda e, s=s, W=W, k0=k0: e.tensor_tensor(
                        out=krot[64:128, k0:k0 + W], in0=t1[s][64:128, :W], in1=t2[s][64:128, :W], op=ALU.add),
                      reads=["t1", "t2"], writes=[("krot", ci)])
                    A("dve", lambda e, pa=pa, s=s, W=W, k0=k0: e.tensor_tensor(
                        out=ckvnT[:, k0:k0 + W], in0=pa[:, :W], in1=rs[s][:, :W], op=ALU.mult),
                      reads=[pr, ("rs", s)], writes=[("ckvn", ci)])
                    if ci == 0:
                        gate_prep_dve()
                    for kc in gate_sched[ci]:
                        gate_step(kc)
                rel_w(3)
                A("dve", lambda e: e.tensor_scalar_mul(out=wukv_bf[:], in0=wukv_st[:], scalar1=gkv_sb[:, 0:1]),
                  reads=["wukv_st", "gkv"], writes=["wukv_bf"])
                for h in range(8):
                    A("sp", dma(kT[h][64:128, :], krot[64:128, :]), reads=[("krot", ci) for ci in range(5)],
                      writes=[("kT", h, ci, 1) for ci in range(5)], dma=True)
                for hf in range(2):
                    A("dve", lambda e, hf=hf: e.tensor_tensor(out=gate_bc[:, hf * 512:(hf + 1) * 512], in0=gpa[hf][0],
                                                              in1=bgate[:, hf * 512:(hf + 1) * 512], op=ALU.add),
                      reads=[gpa[hf][1], "bgate"], writes=[("gate", hf)])
                dump("gate", gate_bc[:], [128, D], F32, [("gate", 0), ("gate", 1)])
                for h in range(8):
                    for ci, (k0, W, isc, n) in enumerate(kchunks):
                        pa, pr = ring6.next()
                        A("pe", lambda e, pa=pa, h=h, W=W, k0=k0: e.matmul(
                            pa[0:64, :W], lhsT=wukv_bf[:, h * 64:(h + 1) * 64], rhs=ckvnT[:, k0:k0 + W],
                            start=True, stop=True), reads=["wukv_bf", ("ckvn", ci)], writes=[pr])
                        if (h + ci) % 2 == 0:
                            A("act", lambda e, pa=pa, h=h, W=W, k0=k0: e.copy(out=kT[h][0:64, k0:k0 + W],
                                                                              in_=pa[0:64, :W]),
                              reads=[pr], writes=[("kT", h, ci, 0)])
                        else:
                            A("dve", lambda e, pa=pa, h=h, W=W, k0=k0: e.tensor_copy(out=kT[h][0:64, k0:k0 + W],
                                                                                     in_=pa[0:64, :W]),
                              reads=[pr], writes=[("kT", h, ci, 0)])
                for kt in range(18):
                    ci = 0 if kt < 2 else 1 + (kt - 2) // 4
                    pa, pr = ring6.next()
                    A("pe", lambda e, pa=pa, kt=kt: e.matmul(pa[:, 0:512], lhsT=ckvnT[:, kt * 128:(kt + 1) * 128],
                                                             rhs=wukv_bf[:, 512:1024], start=True, stop=True),
                      reads=["wukv_bf", ("ckvn", ci)], writes=[pr])
                    dst = Vaug[:, kt, :].rearrange("p (c s d) -> p c s d", c=4, s=3)[:, :, 0:3:2, :]
                    src = pa[:, 0:512].rearrange("p (c t d) -> p c t d", c=4, t=2)
                    if kt % 2 == 0:
                        A("act", lambda e, dst=dst, src=src: e.copy(out=dst, in_=src), reads=[pr], writes=[("V", kt)])
                    else:
                        A("dve", lambda e, dst=dst, src=src: e.tensor_copy(out=dst, in_=src), reads=[pr],
                          writes=[("V", kt)])

                for k in range(2):
                    A("sp", dma(wag[k][:], wuq_d[k * 128:(k + 1) * 128, :]), writes=[("wag", k)], dma=True)
                    A("act", lambda e, k=k: e.activation(out=wuq_bf[:, k, :], in_=wag[k][:], func=AF.Identity,
                                                         scale=gq_sb[:, k:k + 1]),
                      reads=[("wag", k), "gq"], writes=[("wuq_bf", k)])
                dump("ckvnT", ckvnT[:], [128, NKEY], BF16, [("ckvn", ci) for ci in range(5)])
                dump("kT0", kT[0][:], [128, NKEY], BF16, [("kT", 0, ci, j) for ci in range(5) for j in range(2)])
                dump("kT3", kT[3][:], [128, NKEY], BF16, [("kT", 3, ci, j) for ci in range(5) for j in range(2)])
                dump("Vaug", Vaug[:], [128, 18, 768], BF16, [("V", kt) for kt in range(18)])
                if "stopKV" in debug:
                    A("sp", None, deps=list(dbg_outs.values()) + list(P.last_op.values()) + P.pending_dmas)
                    return nc
                P.barrier(scratch[:, 0:1])

            qT = [sb(s1, "qT%d" % h, [128, NTOK], BF16) for h in range(8)]
            with ExitStack() as sq_:
                cqnT = sb(sq_, "cqnT", [128, 2, NTOK], BF16)
                Tq = sb(sq_, "Tq", [128, NTOK], F32)
                sq = [sb(sq_, "sqq%d" % i, [128, 512], BF16) for i in range(2)]
                rs = [sb(sq_, "rsq%d" % i, [128, 512], F32) for i in range(2)]
                banks = Ring(list(all_banks))

                A("sp", dma(Tq[:], tabq_d), writes=["Tq"], dma=True)
                b0 = next_w()
                b1 = next_w()
                def q_finish(n, pa0, pr0, pa1, pr1):
                    s0_, s1_ = 0, 1
                    pa2, pr2 = banks.next()

                    def fn(e):
                        e.matmul(pa2, lhsT=ones_bf[:], rhs=sq[s0_][:], start=True, stop=False)
                        return e.matmul(pa2, lhsT=ones_bf[:], rhs=sq[s1_][:], start=False, stop=True)

                    A("pe", fn, reads=[("sq", s0_), ("sq", s1_), "ones"], writes=[pr2])
                    r = n % 2
                    A("act", lambda e: e.activation(out=rs[r][:], in_=pa2, func=AF.Ln,
                                                    bias=consts[:, 0:1], scale=1.0 / 256.0),
                      reads=[pr2, "consts"], writes=[("rs", r)])
                    A("act", lambda e: e.activation(out=rs[r][:], in_=rs[r][:], func=AF.Exp, scale=-0.5),
                      reads=[("rs", r)], writes=[("rs", r)])
                    A("dve", lambda e: e.tensor_tensor(
                        out=cqnT[:, 0, n * 512:(n + 1) * 512], in0=pa0, in1=rs[r][:], op=ALU.mult),
                      reads=[pr0, ("rs", r)], writes=[("cqn", 0, n)])
                    A("dve", lambda e: e.tensor_tensor(
                        out=cqnT[:, 1, n * 512:(n + 1) * 512], in0=pa1, in1=rs[r][:], op=ALU.mult),
                      reads=[pr1, ("rs", r)], writes=[("cqn", 1, n)])

                pend = None
                for n in range(4):
                    pa0, pr0 = banks.next()
                    px_mm(b0, n, pa0, pr0)
                    pa1, pr1 = banks.next()
                    px_mm(b1, n, pa1, pr1)
                    if pend is not None:
                        q_finish(*pend)
                    A("act", lambda e, pa0=pa0: e.activation(out=sq[0][:], in_=pa0, func=AF.Square),
                      reads=[pr0], writes=[("sq", 0)])
                    A("act", lambda e, pa1=pa1: e.activation(out=sq[1][:], in_=pa1, func=AF.Square),
                      reads=[pr1], writes=[("sq", 1)])
                    pend = (n, pa0, pr0, pa1, pr1)
                q_finish(*pend)
                rel_w(2)
                for h in range(8):
                    for n in range(4):
                        pa, pr = banks.next()

                        def fn(e, pa=pa, h=h, n=n):
                            e.matmul(pa, lhsT=wuq_bf[:, 0, h * 128:(h + 1) * 128], rhs=cqnT[:, 0, n * 512:(n + 1) * 512],
                                     start=True, stop=False)
                            return e.matmul(pa, lhsT=wuq_bf[:, 1, h * 128:(h + 1) * 128],
                                            rhs=cqnT[:, 1, n * 512:(n + 1) * 512], start=False, stop=True)

                        A("pe", fn, reads=[("wuq_bf", 0), ("wuq_bf", 1), ("cqn", 0, n), ("cqn", 1, n)], writes=[pr])
                        A("dve", lambda e, pa=pa, h=h, n=n: e.tensor_tensor(
                            out=qT[h][:, n * 512:(n + 1) * 512], in0=pa, in1=Tq[:, n * 512:(n + 1) * 512], op=ALU.mult),
                          reads=[pr, "Tq"], writes=[("qT", h, n)])
                dump("qT0", qT[0][:], [128, NTOK], BF16, [("qT", 0, n) for n in range(4)])
                dump("qT5", qT[5][:], [128, NTOK], BF16, [("qT", 5, n) for n in range(4)])
                if "stopQ" in debug:
                    A("sp", None, deps=list(dbg_outs.values()) + list(P.last_op.values()) + P.pending_dmas)
                    return nc
                P.barrier(scratch[:, 0:1])

            with ExitStack() as sa:
                NPT = 4
                pt = [sb(sa, "pt%d" % i, [128, 1024], BF16) for i in range(NPT)]
                num = [sb(sa, "num%d" % i, [128, 512], F32) for i in range(2)]
                rden = [sb(sa, "rden%d" % i, [128, 512], F32) for i in range(2)]
                rdsw = [sb(sa, "rdsw%d" % i, [128, 512], F32) for i in range(2)]
                NSP = 3
                Sps = [pb_t[i] for i in range(NSP)]
                accp = [pb_t[3]]

                steps = [(qc, c, hh, ktp) for qc in range(4) for c in range(4) for hh in range(2) for ktp in range(9)]

                def emit_qk(i):
                    qc, c, hh, ktp = steps[i]
                    h = 2 * c + hh
                    s = i % NSP

                    def fn(e):
                        for j in range(2):
                            kt = 2 * ktp + j
                            ins = e.matmul(Sps[s][:, j * 512:(j + 1) * 512], lhsT=kT[h][:, kt * 128:(kt + 1) * 128],
                                           rhs=qT[h][:, qc * 512:(qc + 1) * 512], start=True, stop=True)
                        return ins

                    A("pe", fn, reads=[("kT", h, ci, hf) for ci in range(5) for hf in range(2)] + [("qT", h, qc)],
                      writes=[("bank", 2 * s), ("bank", 2 * s + 1)])

                deferred = []

                def emit_rest(i):
                    qc, c, hh, ktp = steps[i]
                    h = 2 * c + hh
                    s = i % NSP
                    p = i % NPT
                    pair = 0
                    rslot = (qc * 4 + c) % 2
                    A("act", lambda e: e.activation(out=pt[p][:], in_=Sps[s][:], func=AF.Exp, scale=SM_SCALE),
                      reads=[("bank", 2 * s), ("bank", 2 * s + 1)], writes=[("pt", p)])
                    acc = accp[pair][:, hh * 512:(hh + 1) * 512]
                    vbase = c * 192 + hh * 64

                    def fn(e):
                        for j in range(2):
                            kt = 2 * ktp + j
                            ins = e.matmul(acc, lhsT=Vaug[:, kt, vbase:vbase + 128], rhs=pt[p][:, j * 512:(j + 1) * 512],
                                           start=(kt == 0), stop=(kt == 17))
                        return ins

                    A("pe", fn, reads=[("pt", p), ("V", 2 * ktp), ("V", 2 * ktp + 1)], writes=[("bank", 6 + hh)])
                    if ktp == 8:
                        r = rslot
                        nlo, nhi = (0, 64) if hh == 0 else (64, 128)
                        dlo, dhi = (64, 128) if hh == 0 else (0, 64)
                        A("dve", lambda e: e.reciprocal(out=rden[r][dlo:dhi, :], in_=acc[dlo:dhi, :]),
                          reads=[("bank", 6 + hh)], writes=[("rden", r, hh)])
                        A("dve", lambda e: e.tensor_copy(out=num[r][nlo:nhi, :], in_=acc[nlo:nhi, :]),
                          reads=[("bank", 6 + hh)], writes=[("num", r, hh)])
                        A("sp", dma(rdsw[r][nlo:nhi, :], rden[r][dlo:dhi, :]), reads=[("rden", r, hh)],
                          writes=[("rdsw", r, hh)], dma=True)
                        if hh == 1:
                            deferred.append((i + 4, r, c, qc))

                def flush_deferred(i, force=False):
                    while deferred and (force or deferred[0][0] <= i):
                        _, r, c, qc = deferred.pop(0)
                        A("dve", lambda e, r=r, c=c, qc=qc: e.tensor_tensor(
                            out=attT[:, c, qc * 512:(qc + 1) * 512], in0=num[r][:], in1=rdsw[r][:], op=ALU.mult),
                          reads=[("num", r, 0), ("num", r, 1), ("rdsw", r, 0), ("rdsw", r, 1)],
                          writes=[("attT", c, qc)])

                emit_qk(0)
                emit_qk(1)
                for i in range(len(steps)):
                    if i + 2 < len(steps):
                        emit_qk(i + 2)
                    emit_rest(i)
                    flush_deferred(i)
                flush_deferred(0, force=True)
                dump("attT", attT[:], [128, 4, NTOK], BF16, [("attT", c, qc) for c in range(4) for qc in range(4)])
                P.barrier(scratch[:, 0:1])

        with ExitStack() as s2:
            ycv = sb(s2, "ycv", [128, 4, NTOK], BF16)
            merged = sb(s2, "merged", [128, 8, NTOK], BF16)
            woc_bf = sb(s2, "woc_bf", [128, 4, D], BF16)
            wom_bf = sb(s2, "wom_bf", [128, 4, D], BF16)
            wo_bf = sb(s2, "wo_bf", [128, 8, D], BF16)
            stg = [sb(s2, "stg%d" % i, [128, D], F32) for i in range(2)]
            stg_r = Ring([0, 1])
            Sps = [pb_t[i] for i in range(NSP)]
            accp = [pb_t[3]]
            wprep = []
            for (wd, wb, nm) in ((woc_d, woc_bf, "woc"), (wom_d, wom_bf, "wom")):
                for k in range(4):
                    def f(wd=wd, wb=wb, nm=nm, k=k):
                        s = stg_r.next()
                        A("sp", dma(stg[s][:], wd[k * 128:(k + 1) * 128, :]), writes=[("stg", s)], dma=True)
                        A("pool", lambda e: e.tensor_copy(out=wb[:, k, :], in_=stg[s][:]),
                          reads=[("stg", s)], writes=[(nm, k)])
                    wprep.append(f)
            for k in range(8):
                def f(k=k):
                    s = stg_r.next()
                    A("sp", dma(stg[s][:], wo_d[k * 128:(k + 1) * 128, :]), writes=[("stg", s)], dma=True)
                    A("pool", lambda e: e.tensor_tensor(out=wo_bf[:, k, :], in0=stg[s][:], in1=gate_bc[:], op=ALU.mult),
                      reads=[("stg", s)], writes=[("wo", k)])
                wprep.append(f)

            def _hook():
                if wprep:
                    wprep.pop(0)()

            ws_state["hook"] = _hook
            ws_state["cast"] = "act"

            with ExitStack() as sd1:
                sgm = [sb(sd1, "sgm%d" % i, [128, 512], F32) for i in range(2)]
                xcs = sb(sd1, "xcs", [128, NTOK], F32)
                u = sb(sd1, "u", [128, NTOK + 2], F32)
                v = sb(sd1, "v", [128, NTOK], F32)
                A("pool", lambda e: e.memset(u[:, 0:1], 0.0), writes=[("u", "l")])
                A("pool", lambda e: e.memset(u[:, NTOK + 1:NTOK + 2], 0.0), writes=[("u", "r")])
                si = 0
                d1_state = {"held": 0}

                def take1():
                    if d1_state["held"]:
                        rel_w(1)
                    d1_state["held"] = 1
                    return next_w()

                for c in range(4):
                    b = take1()
                    for n in range(4):
                        pa, pr = banks.next()
                        px_mm(b, n, pa, pr)
                        s = si % 2
                        si += 1
                        A("act", lambda e, pa=pa, s=s: e.activation(out=sgm[s][:], in_=pa, func=AF.Silu),
                          reads=[pr], writes=[("sgm", s)])
                        A("dve", lambda e, s=s, c=c, n=n: e.tensor_tensor(
                            out=attT[:, c, n * 512:(n + 1) * 512], in0=attT[:, c, n * 512:(n + 1) * 512], in1=sgm[s][:],
                            op=ALU.mult), reads=[("sgm", s), ("attT", c, n)], writes=[("attT", c, n)])
                for j in range(4):
                    b = take1()
                    for n in range(4):
                        pa, pr = banks.next()
                        px_mm(b, n, pa, pr)
                        A("act", lambda e, pa=pa, n=n: e.copy(out=xcs[:, n * 512:(n + 1) * 512], in_=pa),
                          reads=[pr], writes=[("xcs", n)])
                    b = take1()
                    for n in range(4):
                        pa, pr = banks.next()
                        px_mm(b, n, pa, pr)
                        A("dve", lambda e, pa=pa, n=n: e.tensor_tensor(
                            out=u[:, 1 + n * 512:1 + (n + 1) * 512], in0=pa, in1=xcs[:, n * 512:(n + 1) * 512],
                            op=ALU.mult), reads=[pr, ("xcs", n)], writes=[("u", n)])
                    ures = [("u", n) for n in range(4)] + [("u", "l"), ("u", "r")]
                    A("dve", lambda e, j=j: e.tensor_scalar_mul(out=v[:], in0=u[:, 0:NTOK],
                                                                scalar1=convw_sb[:, 3 * j:3 * j + 1]), reads=ures + ["convw"], writes=["v"])
                    A("dve", lambda e, j=j: e.scalar_tensor_tensor(out=v[:], in0=u[:, 1:NTOK + 1],
                                                                   scalar=convw_sb[:, 3 * j + 1:3 * j + 2], in1=v[:],
                                                                   op0=ALU.mult, op1=ALU.add),
                      reads=ures + ["convw", "v"], writes=["v"])
                    A("dve", lambda e, j=j: e.scalar_tensor_tensor(out=v[:], in0=u[:, 2:NTOK + 2],
                                                                   scalar=convw_sb[:, 3 * j + 2:3 * j + 3], in1=v[:],
                                                                   op0=ALU.mult, op1=ALU.add),
                      reads=ures + ["convw", "v"], writes=["v"])
                    b = take1()
                    for n in range(4):
                        pa, pr = banks.next()
                        px_mm(b, n, pa, pr)
                        A("dve", lambda e, pa=pa, n=n: e.tensor_tensor(
                            out=v[:, n * 512:(n + 1) * 512], in0=pa, in1=v[:, n * 512:(n + 1) * 512], op=ALU.mult),
                          reads=[pr, "v"], writes=[("v2", n)])
                    b = take1()
                    for n in range(4):
                        pa, pr = banks.next()
                        px_mm(b, n, pa, pr)
                        s = si % 2
                        si += 1
                        A("act", lambda e, pa=pa, s=s: e.activation(out=sgm[s][:], in_=pa, func=AF.Silu),
                          reads=[pr], writes=[("sgm", s)])
                        A("dve", lambda e, s=s, j=j, n=n: e.tensor_tensor(
                            out=ycv[:, j, n * 512:(n + 1) * 512], in0=sgm[s][:], in1=v[:, n * 512:(n + 1) * 512],
                            op=ALU.mult), reads=[("sgm", s), ("v2", n)], writes=[("ycv", j, n), "v"])
                rel_w(1)
                dump("attg", attT[:], [128, 4, NTOK], BF16, [("attT", c, qc) for c in range(4) for qc in range(4)])
                dump("ycv", ycv[:], [128, 4, NTOK], BF16, [("ycv", j, n) for j in range(4) for n in range(4)])
                P.barrier(scratch[:, 0:1])

            with ExitStack() as sd2:
                sgc = [sb(sd2, "sgc%d" % i, [128, 512], F32) for i in range(2)]
                sgl = [sb(sd2, "sgl%d" % i, [128, 512], F32) for i in range(2)]
                ta = [sb(sd2, "ta%d" % i, [128, 512], F32) for i in range(2)]
                tb = [sb(sd2, "tb%d" % i, [128, 512], F32) for i in range(2)]
                si = 0
                for i in range(8):
                    bc_ = next_w()
                    bm_ = next_w()
                    for n in range(4):
                        s = si % 2
                        si += 1
                        pa, pr = banks.next()
                        px_mm(bc_, n, pa, pr)
                        A("act", lambda e, pa=pa, s=s: e.activation(out=sgc[s][:], in_=pa, func=AF.Sigmoid),
                          reads=[pr], writes=[("sgc", s)])
                        pa, pr = banks.next()
                        px_mm(bm_, n, pa, pr)
                        A("act", lambda e, pa=pa, s=s: e.activation(out=sgl[s][:], in_=pa, func=AF.Sigmoid),
                          reads=[pr], writes=[("sgl", s)])
                        pa, pr = banks.next()

                        def fn(e, pa=pa, i=i, n=n):
                            for j in range(4):
                                ins = e.matmul(pa, lhsT=woc_bf[:, j, i * 128:(i + 1) * 128],
                                               rhs=ycv[:, j, n * 512:(n + 1) * 512], start=(j == 0), stop=(j == 3))
                            return ins

                        A("pe", fn, reads=[("woc", j) for j in range(4)] + [("ycv", j, n) for j in range(4)],
                          writes=[pr])
                        A("dve", lambda e, pa=pa, s=s: e.tensor_tensor(out=ta[s][:], in0=pa, in1=sgc[s][:],
                                                                       op=ALU.mult),
                          reads=[pr, ("sgc", s)], writes=[("ta", s)])
                        pa, pr = banks.next()

                        def fn(e, pa=pa, i=i, n=n):
                            for j in range(4):
                                ins = e.matmul(pa, lhsT=wom_bf[:, j, i * 128:(i + 1) * 128],
                                               rhs=attT[:, j, n * 512:(n + 1) * 512], start=(j == 0), stop=(j == 3))
                            return ins

                        A("pe", fn, reads=[("wom", j) for j in range(4)] + [("attT", j, n) for j in range(4)],
                          writes=[pr])
                        A("dve", lambda e, pa=pa, s=s: e.tensor_tensor(out=tb[s][:], in0=pa, in1=sgl[s][:],
                                                                       op=ALU.mult),
                          reads=[pr, ("sgl", s)], writes=[("tb", s)])
                        A("dve", lambda e, s=s, i=i, n=n: e.tensor_tensor(
                            out=merged[:, i, n * 512:(n + 1) * 512], in0=ta[s][:], in1=tb[s][:], op=ALU.add),
                          reads=[("ta", s), ("tb", s)], writes=[("merged", i, n)])
                    rel_w(2)
                dump("merged", merged[:], [128, 8, NTOK], BF16, [("merged", i, n) for i in range(8) for n in range(4)])
                P.barrier(scratch[:, 0:1])

            with ExitStack() as sd3:
                NS = 4
                xs2 = [sb(sd3, "xs2_%d" % i, [128, D], F32) for i in range(NS)]
                rr = [sb(sd3, "rr%d" % i, [128, D], F32) for i in range(NS)]
                lng = sb(sd3, "lng", [128, D], F32)
                lnb = sb(sd3, "lnb", [128, D], F32)
                st6 = [sb(sd3, "st6_%d" % i, [128, 12], F32) for i in range(NS)]
                mv = [sb(sd3, "mv%d" % i, [128, 8], F32) for i in range(NS)]
                junk = stg[0]
                A("sp", dma(lng[:], lng_d.partition_broadcast(128)), writes=["lng"], dma=True)
                A("sp", dma(lnb[:], lnb_d.partition_broadcast(128)), writes=["lnb"], dma=True)
                out_ops = []

                def load_x(tt):
                    A("sp", dma(xs2[tt % NS][:], x_d[tt * 128:(tt + 1) * 128, :]), writes=[("xs2", tt % NS)], dma=True)

                for tt in range(NS):
                    load_x(tt)

                def early(tt):
                    s = tt % NS
                    pa = pb_t[tt % 4]
                    pr = ("bank", 2 * (tt % 4))
                    pr2 = ("bank", 2 * (tt % 4) + 1)

                    def fn(e):
                        for hf in range(2):
                            for kc in range(8):
                                ins = e.matmul(pa[:, hf * 512:(hf + 1) * 512], lhsT=merged[:, kc, tt * 128:(tt + 1) * 128],
                                               rhs=wo_bf[:, kc, hf * 512:(hf + 1) * 512], start=(kc == 0), stop=(kc == 7))
                        return ins

                    A("pe", fn, reads=[("merged", i, tt // 4) for i in range(8)] + [("wo", k) for k in range(8)],
                      writes=[pr, pr2])
                    A("dve", lambda e: e.scalar_tensor_tensor(out=rr[s][:], in0=xs2[s][:], scalar=ALPHA,
                                                              in1=pa[:], op0=ALU.mult, op1=ALU.add,
                                                              accum_out=mv[s][:, 0:1]),
                      reads=[pr, pr2, ("xs2", s)], writes=[("rr", s), ("mv", s, 0)])
                    if tt + NS < 16:
                        load_x(tt + NS)
                    A("act", lambda e: e.activation(out=junk[:], in_=rr[s][:], func=AF.Square,
                                                    accum_out=mv[s][:, 1:2]),
                      reads=[("rr", s)], writes=["junk", ("mv", s, 1)])
                    A("dve", lambda e: e.tensor_scalar_mul(out=mv[s][:, 4:5], in0=mv[s][:, 0:1], scalar1=1.0 / D),
                      reads=[("mv", s, 0)], writes=[("mv", s, 4)])
                    A("dve", lambda e: e.scalar_tensor_tensor(out=mv[s][:, 5:6], in0=mv[s][:, 4:5], scalar=-1.0,
                                                              in1=mv[s][:, 4:5], op0=ALU.mult, op1=ALU.mult),
                      reads=[("mv", s, 4)], writes=[("mv", s, 5)])
                    A("dve", lambda e: e.scalar_tensor_tensor(out=mv[s][:, 6:7], in0=mv[s][:, 1:2], scalar=1.0 / D,
                                                              in1=mv[s][:, 5:6], op0=ALU.mult, op1=ALU.add),
                      reads=[("mv", s, 1), ("mv", s, 5)], writes=[("mv", s, 6)])
                    A("act", lambda e: e.activation(out=mv[s][:, 2:3], in_=mv[s][:, 6:7], func=AF.Sqrt,
                                                    bias=consts[:, 1:2], scale=1.0),
                      reads=[("mv", s, 6), "consts"], writes=[("mv2", s)])
                    A("dve", lambda e: e.reciprocal(out=mv[s][:, 2:3], in_=mv[s][:, 2:3]),
                      reads=[("mv2", s)], writes=[("mv2", s)])
                    A("dve", lambda e: e.scalar_tensor_tensor(out=mv[s][:, 3:4], in0=mv[s][:, 4:5], scalar=-1.0,
                                                              in1=mv[s][:, 2:3], op0=ALU.mult, op1=ALU.mult),
                      reads=[("mv", s, 4), ("mv2", s)], writes=[("mv3", s)])
                    A("act", lambda e: e.activation(out=rr[s][:], in_=rr[s][:], func=AF.Identity,
                                                    bias=mv[s][:, 3:4], scale=mv[s][:, 2:3]),
                      reads=[("rr", s), ("mv2", s), ("mv3", s)], writes=[("rr", s)])

                def late(tt):
                    s = tt % NS
                    A("pool", lambda e: e.tensor_tensor(out=rr[s][:], in0=rr[s][:], in1=lng[:], op=ALU.mult),
                      reads=[("rr", s), "lng"], writes=[("rr", s)])
                    A("dve",
                      lambda e: e.tensor_tensor(out=rr[s][:], in0=rr[s][:], in1=lnb[:], op=ALU.add),
                      reads=[("rr", s), "lnb"], writes=[("rr", s)])
                    out_ops.append(A("sp", dma(out_d[tt * 128:(tt + 1) * 128, :], rr[s][:]), reads=[("rr", s)],
                                     dma=True))

                SKEW = 2
                for tt in range(16 + SKEW):
                    if tt < 16:
                        early(tt)
                    if tt - SKEW >= 0:
                        late(tt - SKEW)
                A("sp", None, deps=out_ops + list(dbg_outs.values()))
    return nc


_CACHE = {}


def _rope_tables():
    grid_w = 64
    t = np.arange(NTOK)
    row = (t // grid_w).astype(np.float32)
    col = (t % grid_w).astype(np.float32)
    inv = (np.float32(10000.0) ** (-np.arange(0, 16, 2, dtype=np.float32) / np.float32(16))).astype(np.float32)
    ar = row[:, None] * inv[None, :]
    ac = col[:, None] * inv[None, :]
    cr, sr, cc, sc = np.cos(ar), np.sin(ar), np.cos(ac), np.sin(ac)
    C = np.concatenate([cr, cr, cc, cc], axis=1).astype(np.float32)
    S = np.concatenate([-sr, sr, -sc, sc], axis=1).astype(np.float32)
    return C, S


def _swap32(a):
    return np.concatenate([a[..., 8:16], a[..., 0:8], a[..., 24:32], a[..., 16:24]], axis=-1)


def kernel(x, c, ctx, c_ctx, w_ada, b_ada, w_in, conv_w, q_norm_g, w_uq, kv_norm_g, w_ukv,
           w_out_conv, w_out_mla, w_o, ln_g, ln_b, _debug=()):
    f = np.float32
    x = np.asarray(x, f); c = np.asarray(c, f); ctx = np.asarray(ctx, f); c_ctx = np.asarray(c_ctx, f)
    w_ada = np.ascontiguousarray(np.asarray(w_ada, f)[0]); b_ada = np.asarray(b_ada, f)[0]
    w_in = np.ascontiguousarray(np.asarray(w_in, f)[0]); conv_w = np.asarray(conv_w, f)[0]
    q_norm_g = np.asarray(q_norm_g, f)[0]; w_uq = np.asarray(w_uq, f)[0]
    kv_norm_g = np.asarray(kv_norm_g, f)[0]; w_ukv = np.asarray(w_ukv, f)[0]
    w_out_conv = np.ascontiguousarray(np.asarray(w_out_conv, f)[0])
    w_out_mla = np.ascontiguousarray(np.asarray(w_out_mla, f)[0])
    w_o = np.ascontiguousarray(np.asarray(w_o, f)[0]); ln_g = np.asarray(ln_g, f)[0]; ln_b = np.asarray(ln_b, f)[0]
    B = x.shape[0]

    C, S = _rope_tables()
    tabq = np.ascontiguousarray(np.concatenate([np.ones((64, NTOK), f), C.T, S.T], axis=0))
    Ck = np.concatenate([np.ones((32, NCTX), f), C.T], axis=1)
    Sk = np.concatenate([np.zeros((32, NCTX), f), S.T], axis=1)
    tabk = np.ascontiguousarray(np.stack([np.concatenate([Ck, Ck], 0), np.concatenate([Sk, Sk], 0)], 0))
    kr = w_in[:, 2432:2464]
    krs = _swap32(kr)
    w_kr = np.ascontiguousarray(np.concatenate([kr] * 4 + [krs] * 4, axis=1))
    wq = w_uq.reshape(256, 8, 96)
    w_uq_aug = np.ascontiguousarray(
        np.concatenate([wq[:, :, 0:64], wq[:, :, 64:96], _swap32(wq[:, :, 64:96])], axis=2).reshape(256, 1024))
    wkv = w_ukv.reshape(128, 8, 128)
    w_ukv_r = np.ascontiguousarray(
        np.concatenate([wkv[:, :, 0:64].reshape(128, 512), wkv[:, :, 64:128].reshape(128, 512)], axis=1))
    bada = b_ada[0:2048].reshape(16, 128).T
    bada2 = np.ascontiguousarray(np.repeat(bada[:, :, None], 2, axis=2).reshape(128, 32))
    badag = np.ascontiguousarray(b_ada[2048:3072])
    convw = np.ascontiguousarray(conv_w.reshape(3, 4, 128).transpose(2, 1, 0).reshape(128, 12))
    gq = np.ascontiguousarray(q_norm_g.reshape(2, 128).T)
    gkv = np.ascontiguousarray(kv_norm_g.reshape(1, 128).T)
    cctx_cols = c_ctx.reshape(8, 128).T

    key = tuple(sorted(_debug))
    if key not in _CACHE:
        _CACHE[key] = build_program(debug=_debug)
    nc = _CACHE[key]
    in_maps = []
    for b in range(B):
        cc = np.stack([c[b].reshape(8, 128).T, cctx_cols], axis=2).reshape(128, 16)
        in_maps.append({
            "x": np.ascontiguousarray(x[b]), "ctx": np.ascontiguousarray(ctx[b]), "cc": np.ascontiguousarray(cc),
            "w_ada": w_ada, "bada": bada2, "badag": badag, "w_in": w_in, "w_kr": w_kr, "convw": convw,
            "gq": gq, "gkv": gkv, "w_uq": w_uq_aug, "w_ukv": w_ukv_r, "w_oc": w_out_conv, "w_om": w_out_mla,
            "w_o": w_o, "ln_g": ln_g, "ln_b": ln_b, "tabq": tabq, "tabk": tabk,
        })
    res = run_bass_kernel_spmd(nc, in_maps, core_ids=list(range(B)))
    out = np.stack([np.asarray(r["out"], f) for r in res.results], axis=0)
    if _debug:
        return out, res.results
    return out
```
